# Optimizing a Trainium2 kernel written in Bass

```python
import jax, jax.numpy as jnp
from jax import lax
import numpy as np

D_MODEL = 2048
BATCH = 4
SEQ = 4096
DEPTH = 1

D_MIX = D_MODEL
D_GMLP = D_MIX // 2
D_ATTN = D_MIX - D_GMLP
CHUNK = 128
GMLP_GROUPS = 8
GMLP_GROUP_DIM = D_GMLP // GMLP_GROUPS
HEAD_DIM = 64
N_Q_HEADS = D_ATTN // HEAD_DIM
N_KV_HEADS = 2
GQA_GROUP = N_Q_HEADS // N_KV_HEADS
D_KV = N_KV_HEADS * HEAD_DIM
WINDOW = 128
BLOCK = 128
ROPE_THETA = 10000.0
EPS = 1e-6

_SIZES = [D_GMLP, D_GMLP, D_GMLP, D_ATTN, D_KV, D_KV, D_ATTN]
_OFFS = list(np.cumsum(_SIZES)[:-1].tolist())
D_IN_PROJ = int(sum(_SIZES))
D_QKV = D_ATTN + 2 * D_KV
QKV_START = 3 * D_GMLP

kernel_name = "hybrid_gmlp_swa_sink_layer"


def rms_norm(x, g):
    xf = x.astype(jnp.float32)
    y = xf * lax.rsqrt(jnp.mean(xf * xf, axis=-1, keepdims=True) + EPS)
    return (y * g.astype(jnp.float32)).astype(x.dtype)


def layer_norm(x, g, b):
    xf = x.astype(jnp.float32)
    mu = jnp.mean(xf, axis=-1, keepdims=True)
    var = jnp.mean(jnp.square(xf - mu), axis=-1, keepdims=True)
    y = (xf - mu) * lax.rsqrt(var + EPS) * g.astype(jnp.float32) + b.astype(jnp.float32)
    return y.astype(x.dtype)


def rope(x, positions):
    half = HEAD_DIM // 2
    inv_freq = ROPE_THETA ** (-jnp.arange(half, dtype=jnp.float32) * (2.0 / HEAD_DIM))
    ang = positions.astype(jnp.float32)[..., None] * inv_freq
    cos = jnp.cos(ang)[:, :, None, :]
    sin = jnp.sin(ang)[:, :, None, :]
    xf = x.astype(jnp.float32)
    x1, x2 = xf[..., :half], xf[..., half:]
    return jnp.concatenate([x1 * cos - x2 * sin, x2 * cos + x1 * sin], axis=-1).astype(x.dtype)


def chunked_spatial_gating(u, v, ln_g, ln_b, w_s, b_s):
    B, S, _ = u.shape
    nc = S // CHUNK
    vn = layer_norm(v, ln_g, ln_b).reshape(B, nc, CHUNK, GMLP_GROUPS, GMLP_GROUP_DIM)
    causal = jnp.tril(jnp.ones((CHUNK, CHUNK), dtype=bool))
    w = w_s * causal.astype(w_s.dtype)
    mixed = jnp.einsum('gts,bnsgc->bntgc', w, vn)
    mixed = mixed + jnp.transpose(b_s)[None, None, :, :, None]
    return u * mixed.reshape(B, S, D_GMLP)


def sliding_window_attention(q, k, v, sinks):
    B, S, _, _ = q.shape
    nb = S // BLOCK
    qb = q.reshape(B, nb, BLOCK, N_KV_HEADS, GQA_GROUP, HEAD_DIM)
    kb = k.reshape(B, nb, BLOCK, N_KV_HEADS, HEAD_DIM)
    vb = v.reshape(B, nb, BLOCK, N_KV_HEADS, HEAD_DIM)
    pad = ((0, 0), (1, 0), (0, 0), (0, 0), (0, 0))
    kk = jnp.concatenate([jnp.pad(kb, pad)[:, :-1], kb], axis=2)
    vv = jnp.concatenate([jnp.pad(vb, pad)[:, :-1], vb], axis=2)
    scores = jnp.einsum('bnqhgd,bnkhd->bnhgqk', qb, kk,
                        preferred_element_type=jnp.float32) * (HEAD_DIM ** -0.5)
    qi = jnp.arange(BLOCK)[:, None] + BLOCK
    kj = jnp.arange(2 * BLOCK)[None, :]
    dist = qi - kj
    band = (dist >= 0) & (dist < WINDOW)
    blk = jnp.arange(nb)[:, None, None]
    valid = band[None] & ((blk * BLOCK + kj[None] - BLOCK) >= 0)
    scores = jnp.where(valid[None, :, None, None], scores, -jnp.inf)
    sink = sinks.astype(jnp.float32).reshape(N_KV_HEADS, GQA_GROUP)[None, None, :, :, None, None]
    m = jnp.maximum(jnp.max(scores, axis=-1, keepdims=True), sink)
    p = jnp.exp(scores - m)
    denom = jnp.sum(p, axis=-1, keepdims=True) + jnp.exp(sink - m)
    p = (p / denom).astype(v.dtype)
    out = jnp.einsum('bnhgqk,bnkhd->bnqhgd', p, vv)
    return out.reshape(B, S, N_Q_HEADS * HEAD_DIM)


def setup_inputs(seed: int = 0) -> dict:
    key = jax.random.key(seed)
    ks = jax.random.split(key, 12)
    f32 = jnp.float32
    x = jax.random.normal(ks[0], (BATCH, SEQ, D_MODEL), f32)
    offsets = jax.random.randint(ks[1], (BATCH, 1), 0, 1024, dtype=jnp.int32)
    positions = (jnp.arange(SEQ, dtype=jnp.int32)[None, :] + offsets).astype(jnp.int32)
    g_pre = 1.0 + 0.02 * jax.random.normal(ks[2], (DEPTH, D_MODEL), f32)
    w_in = jax.random.normal(ks[3], (DEPTH, D_MODEL, D_IN_PROJ), f32) * (D_MODEL ** -0.5)
    b_qkv = 0.01 * jax.random.normal(ks[4], (DEPTH, D_QKV), f32)
    ln_v_g = 1.0 + 0.02 * jax.random.normal(ks[5], (DEPTH, D_GMLP), f32)
    ln_v_b = 0.01 * jax.random.normal(ks[6], (DEPTH, D_GMLP), f32)
    w_spatial = jax.random.normal(ks[7], (DEPTH, GMLP_GROUPS, CHUNK, CHUNK), f32) * (CHUNK ** -0.5)
    b_spatial = 1.0 + 0.02 * jax.random.normal(ks[8], (DEPTH, GMLP_GROUPS, CHUNK), f32)
    attn_sinks = jax.random.normal(ks[9], (DEPTH, N_Q_HEADS), f32)
    w_out = jax.random.normal(ks[10], (DEPTH, D_MIX, D_MODEL), f32) * (D_MIX ** -0.5)
    g_post = 1.0 + 0.02 * jax.random.normal(ks[11], (DEPTH, D_MODEL), f32)
    return {"x": x, "positions": positions, "g_pre": g_pre, "w_in": w_in, "b_qkv": b_qkv,
            "ln_v_g": ln_v_g, "ln_v_b": ln_v_b, "w_spatial": w_spatial, "b_spatial": b_spatial,
            "attn_sinks": attn_sinks, "w_out": w_out, "g_post": g_post}


def reference(x, positions, g_pre, w_in, b_qkv, ln_v_g, ln_v_b, w_spatial, b_spatial,
              attn_sinks, w_out, g_post):
    B, S, _ = x.shape
    for l in range(DEPTH):
        h = rms_norm(x, g_pre[l])
        proj = jnp.einsum('bsd,de->bse', h, w_in[l])
        bias = jnp.concatenate([jnp.zeros((QKV_START,), proj.dtype), b_qkv[l].astype(proj.dtype),
                                jnp.zeros((D_ATTN,), proj.dtype)])
        proj = proj + bias
        u, v_g, z_a, q, k, v_a, z_b = jnp.split(proj, _OFFS, axis=-1)
        y_a = chunked_spatial_gating(u, v_g, ln_v_g[l], ln_v_b[l], w_spatial[l], b_spatial[l])
        y_a = y_a * jax.nn.silu(z_a)
        q = rope(q.reshape(B, S, N_Q_HEADS, HEAD_DIM), positions)
        k = rope(k.reshape(B, S, N_KV_HEADS, HEAD_DIM), positions)
        v_a = v_a.reshape(B, S, N_KV_HEADS, HEAD_DIM)
        y_b = sliding_window_attention(q, k, v_a, attn_sinks[l]) * jax.nn.silu(z_b)
        y = jnp.einsum('bse,ed->bsd', jnp.concatenate([y_a, y_b], axis=-1), w_out[l])
        x = x + rms_norm(y, g_post[l])
    return x
```

```python
import numpy as np
from contextlib import ExitStack

import concourse.bass as bass
import concourse.mybir as mybir
from concourse.bass_utils import run_bass_kernel_spmd

F32 = mybir.dt.float32
BF16 = mybir.dt.bfloat16
I32 = mybir.dt.int32
ALU = mybir.AluOpType
AF = mybir.ActivationFunctionType
AX = mybir.AxisListType

D = 2048
NTOK = 2048
HALO = 128
NPASS = 2
PT = 1024
PTH = PT + HALO
NT = PT // 128
E_IN = 5376
EPS = 1e-6
UW = 256
NRING = 4
NSTG = 2
MASKNEG = 2000.0


class Eng:
    def __init__(self, name, sem):
        self.name = name
        self.sem = sem
        self.count = 0
        self.waited = {}
        self.ops = []


class Buf:
    def __init__(self, name="", excl=False):
        self.name = name
        self.w = None
        self.r = []
        self.excl = excl


class FW:
    def __init__(self, nc, sems):
        self.nc = nc
        self.E = {n: Eng(n, sems[n]) for n in ("pe", "act", "dve", "pool", "sp")}
        self.dma_sems = {}

    def _emit_waits(self, eng, waits):
        need = {}
        for ev in waits:
            if ev is None:
                continue
            sem, val, key = ev
            if eng.waited.get(key, 0) >= val:
                continue
            if need.get(key, (None, 0))[1] < val:
                need[key] = (sem, val)
        out = []
        for key, (sem, val) in need.items():
            eng.waited[key] = val
            out.append((sem, val))
        return out

    def _deps(self, reads, writes, waits):
        evs = list(waits)
        for b in reads:
            if b.w is not None:
                evs.append(b.w)
            if b.excl:
                evs.extend(b.r)
        for b in writes:
            evs.extend(b.r)
            if b.w is not None:
                evs.append(b.w)
        return evs

    def _commit(self, ev, reads, writes):
        for b in reads:
            if b.excl:
                b.w = ev
                b.r = []
            else:
                b.r.append(ev)
        for b in writes:
            b.w = ev
            b.r = []

    def op(self, en, fn, reads=(), writes=(), waits=()):
        eng = self.E[en]
        evs = self._deps(reads, writes, waits)
        if en == "pe":
            evs = [e for e in evs if e is not None and e[2] != "pe"]
        wl = self._emit_waits(eng, evs)
        eng.count += 1
        cnt = eng.count
        sem = eng.sem

        def run(h, fn=fn, wl=wl, sem=sem):
            for (s, v) in wl:
                h.wait_ge(s, v)
            ins = fn(h)
            ins.then_inc(sem, 1)

        eng.ops.append(run)
        ev = (sem, cnt, en)
        self._commit(ev, reads, writes)
        return ev

    def dma(self, en, out, in_, sem_name, reads=(), writes=(), waits=()):
        eng = self.E[en]
        evs = self._deps(reads, writes, waits)
        wl = self._emit_waits(eng, evs)
        sem, cnt = self.dma_sems[sem_name]
        cnt += 16
        self.dma_sems[sem_name] = (sem, cnt)

        def run(h, wl=wl, sem=sem, out=out, in_=in_):
            for (s, v) in wl:
                h.wait_ge(s, v)
            h.dma_start(out=out, in_=in_).then_inc(sem, 16)

        eng.ops.append(run)
        ev = (sem, cnt, "dma:" + sem_name)
        self._commit(ev, reads, writes)
        return ev

    def all_events(self):
        evs = []
        for n, e in self.E.items():
            if e.count > 0:
                evs.append((e.sem, e.count, n))
        for n, (sem, cnt) in self.dma_sems.items():
            if cnt > 0:
                evs.append((sem, cnt, "dma:" + n))
        return evs

    def fence(self, engines=("pe", "act", "dve", "pool", "sp")):
        evs = self.all_events()
        for n in engines:
            eng = self.E[n]
            wl = self._emit_waits(eng, [e for e in evs if e[2] != n])

            def run(h, wl=wl):
                for (s, v) in wl:
                    h.wait_ge(s, v)

            eng.ops.append(run)

    def replay(self, block):
        E = self.E

        @block.sync
        def _(h):
            for f in E["sp"].ops:
                f(h)

        @block.tensor
        def _(h):
            for f in E["pe"].ops:
                f(h)

        @block.scalar
        def _(h):
            for f in E["act"].ops:
                f(h)

        @block.vector
        def _(h):
            for f in E["dve"].ops:
                f(h)

        @block.gpsimd
        def _(h):
            for f in E["pool"].ops:
                f(h)


class _Stop(Exception):
    pass


def build_program(debug=False, stop=None):
    nc = bass.Bass("TRN2", target_bir_lowering=False)

    def din(name, shape, dt=F32):
        return nc.dram_tensor(name, list(shape), dt, kind="ExternalInput").ap()

    x_d = din("x", [NTOK + HALO, D])
    pos_d = din("pos", [1, NTOK + HALO], I32)
    win_d = din("w_in", [D, E_IN])
    wout_d = din("w_out", [D, D])
    gpre_d = din("g_pre", [1, D])
    gpost_d = din("g_post", [1, D])
    cpa_d = din("cpa", [128, 181])
    cpb_d = din("cpb", [128, 3328])
    y_d = nc.dram_tensor("y", [NTOK, D], F32, kind="ExternalOutput").ap()
    dbg = {}
    if debug:
        for nm, shp, dt in [("dbg_hT", [128, 16 * PTH], BF16), ("dbg_vn", [128, 8 * 1024], BF16),
                            ("dbg_KT", [128, 2 * PTH], BF16), ("dbg_yTa", [128, 8 * 1024], BF16),
                            ("dbg_yTb", [128, 8 * 1024], BF16), ("dbg_cs", [128, 2 * PTH], F32)]:
            dbg[nm] = nc.dram_tensor(nm, shp, dt, kind="ExternalOutput").ap()

    with ExitStack() as es:
        def sb(name, shape, dt):
            return es.enter_context(nc.sbuf_tensor(name, list(shape), dt))

        sems = {n: es.enter_context(nc.semaphore("s_" + n)) for n in ("pe", "act", "dve", "pool", "sp")}
        fw = FW(nc, sems)
        for n in ["cst", "cst2", "pos", "gbc", "xt0", "xt1", "xt2", "xt3", "ycp0", "ycp1", "dbg"] + [f"st{i}" for i in range(8)] + [f"stg{i}" for i in range(NSTG)]:
            fw.dma_sems[n] = (es.enter_context(nc.semaphore("d_" + n)), 0)

        arena = sb("arena", [128, 32768], BF16)
        hT = arena[:, 0:16 * PTH].rearrange("p (c t) -> p c t", c=16)
        o1 = 16 * PTH
        KT = arena[:, o1:o1 + 2 * PTH].rearrange("p (k t) -> p k t", k=2)
        o2 = o1 + 2 * PTH
        VAB = arena[:, o2:o2 + 9 * 2 * 2 * 128].rearrange("p (i k a d) -> p i k a d", i=9, k=2, a=2)
        o3 = o2 + 9 * 2 * 2 * 128
        cosT = arena[:, o3:o3 + 2 * PTH].bitcast(F32)
        o4 = o3 + 2 * PTH
        sinT = arena[:, o4:o4 + 2 * PTH].bitcast(F32)
        o5 = o4 + 2 * PTH
        assert o5 <= 32768
        yout = arena[:, :].bitcast(F32).rearrange("p (i d) -> p i d", i=8)

        vn_t = sb("vn", [128, 8 * 1024], BF16)
        vn = vn_t[:, :].rearrange("p (i e) -> p i e", i=8)
        yTb = vn
        yTa_t = sb("yTa", [128, 8 * 1024], BF16)
        yTa = yTa_t[:, :].rearrange("p (i e) -> p i e", i=8)
        vraw = yTa_t[:, :].bitcast(F32).rearrange("p (i e) -> p i e", i=8)
        hb = [vn_t[:, 0:2048], vn_t[:, 2048:4096]]
        sqj = vn_t[:, 4096:6144]

        ring = [sb(f"ring{i}", [128, 16, UW], BF16) for i in range(NRING)]
        stg = [sb(f"stg{i}", [128, 8, UW], F32) for i in range(NSTG)]
        gbc = sb("gbc", [128, D], F32)
        scr = sb("scr", [128, 4608], F32)
        cscr = sb("cscr", [128, 9216], BF16)
        cinit = arena[:, o1:o3].bitcast(F32)
        ident_f = cinit[:, 0:128]
        pswap_f = cinit[:, 128:256]
        mss_f = cinit[:, 256:768]
        mfs_f = cinit[:, 768:1280]
        wst_f = cinit[:, 1280:2304]
        bs_bc = cinit[:, 2304:3328]
        xt = [cscr[:, 0:4096].bitcast(F32), cscr[:, 4096:8192].bitcast(F32), scr[:, 0:2048], scr[:, 2048:4096]]
        pos_t = sb("pos_i", [128, PTH], I32)
        pos_i = pos_t[:, :]
        ang = scr[:, 0:PTH]
        kk = scr[:, PTH:2 * PTH]
        rr = scr[:, 2 * PTH:3 * PTH]
        th = [scr[:, 0:512], scr[:, 512:1024]]
        mg = [scr[:, 1024:1536], scr[:, 1536:2048]]
        gz = [scr[:, 2048:3072], scr[:, 3072:4096]]
        ident_b = sb("ident_b", [128, 128], BF16)
        ones_b = sb("ones_b", [128, 128], BF16)
        sel_b = sb("sel_b", [128, 2, 128], BF16)
        pswap_b = sb("pswap_b", [128, 128], BF16)
        mss_b = sb("mss_b", [128, 512], BF16)
        mfs_b = sb("mfs_b", [128, 512], BF16)
        wst_b = sb("wst_b", [128, 1024], BF16)
        biasg = sb("biasg", [128, 1024], F32)
        cpa = sb("cpa_s", [128, 192], F32)
        bv_bc = cpa[:, 0:128]
        bq_fm = cpa[:, 128:136]
        bk_fm = cpa[:, 136:138]
        lng_fm = cpa[:, 138:146]
        lnb_fm = cpa[:, 146:154]
        sink_fm = cpa[:, 154:162]
        invf = cpa[:, 162:163]
        sgn = cpa[:, 163:164]
        gpre_fm = cpa[:, 164:180]
        bk_nat = cpa[:, 180:181]
        esink = cpa[:, 181:189]
        nhalf = cpa[:, 189:190]
        epsc = cpa[:, 190:191]
        stat = sb("stat", [128, 256], F32)
        ss0 = stat[:, 0:9]
        ms0 = stat[:, 9:18]
        rstd0 = stat[:, 18:27]
        bst = stat[:, 32:32 + 96].rearrange("p (i h s) -> p i h s", i=8, h=2)
        mv = stat[:, 128:144].rearrange("p (i s) -> p i s", i=8)
        vr = stat[:, 144:152]
        rstdv = stat[:, 152:160]
        ssq = stat[:, 160:224].rearrange("p (i v) -> p i v", i=8)
        ssd = stat[:, 224:232]
        msd = stat[:, 232:240]
        rstd2 = stat[:, 240:248]
        nmr = stat[:, 248:256]
        qf = [cscr[:, 0:1024].bitcast(F32), cscr[:, 1024:2048].bitcast(F32)]
        qtr = [cscr[:, 2048:3072], cscr[:, 3072:4096]]
        qb = [cscr[:, 4096:4608], cscr[:, 4608:5120]]
        ptA = [cscr[:, 5120:5632], cscr[:, 5632:6144]]
        ptB = [cscr[:, 6144:6656], cscr[:, 6656:7168]]
        dn = [cscr[:, 7168:7680].bitcast(F32), cscr[:, 7680:8192].bitcast(F32)]
        yn = [cscr[:, 8192:8704].bitcast(F32), cscr[:, 8704:9216].bitcast(F32)]
        sqj2 = sb("sqj2", [128, 256], BF16)
        t1 = qf
        t2 = mg
        az = th
        rd = dn

        pb = [es.enter_context(nc.psum_tensor(f"pb{i}", [128, 512], F32)) for i in range(8)]

        def mk(n, k=None):
            if k is None:
                return Buf(n)
            return [Buf(f"{n}{i}") for i in range(k)]

        B_pb = [Buf(f"pb{i}", excl=True) for i in range(8)]
        B_ring = mk("ring", NRING)
        B_stg = mk("stg", NSTG)
        B_cst = Buf("cst")
        B_gbc = Buf("gbc")

        B_xt = mk("xt", 2); B_hb = mk("hb", 2); B_sq = Buf("sq"); B_st = mk("stat0", 9); B_hT = mk("hT", 9)
        B_rope = Buf("rope"); B_VAB = Buf("VAB"); B_vraw = Buf("vraw"); B_vst = mk("vst", 8); B_vn = mk("vn", 8)
        B_yTb = B_vn
        B_KT = Buf("KT"); B_qf = mk("qf", 2); B_kf = B_qf; B_t1 = B_qf; B_qb = mk("qb", 2); B_kb = B_qb
        B_th = mk("th", 2); B_az = B_th; B_mg = mk("mg", 2); B_t2 = B_mg; B_yTa = mk("yTa", 8)
        B_qtr = mk("qtr", 2); B_gz = mk("gz", 2); B_ptA = mk("ptA", 2); B_ptB = mk("ptB", 2)
        B_dn = mk("dn", 2); B_rd = B_dn; B_yn = mk("yn", 2); B_yout = mk("yout", 8); B_ssq = mk("ssq", 2); B_ssq8 = mk("ssq8", 8)
        B_sqj2 = Buf("sqj2"); B_xr = mk("xr", 2); B_yo2 = mk("yo2", 8)

        B_c2 = Buf("c2")
        B_cid = Buf("cid")
        B_eps = Buf("eps")
        B_es = Buf("es")
        B_sel = Buf("sel")
        B_init = Buf("init")

        def emit_init():
            fw.op("pool", lambda h: h.memset(nhalf, -0.5), reads=[], writes=[B_eps])
            fw.op("pool", lambda h: h.memset(epsc, EPS), reads=[], writes=[B_eps])
            fw.dma("sp", cpa[:, 0:181], cpa_d, "cst", writes=[B_cst])
            fw.dma("sp", cinit[:, 0:3328], cpb_d, "cst2", writes=[B_init])
            fw.op("pool", lambda h: h.tensor_copy(out=ident_b[:], in_=ident_f), reads=[B_init], writes=[B_cid])

        def emit_init_late():
            fw.op("pool", lambda h: h.tensor_copy(out=pswap_b[:], in_=pswap_f), reads=[B_init], writes=[B_c2])
            fw.op("pool", lambda h: h.tensor_scalar(out=mss_b[:], in0=mss_f, scalar1=-1.0, scalar2=MASKNEG, op0=ALU.add, op1=ALU.mult), reads=[B_init], writes=[B_c2])
            fw.op("pool", lambda h: h.tensor_scalar(out=mfs_b[:], in0=mfs_f, scalar1=-1.0, scalar2=MASKNEG, op0=ALU.add, op1=ALU.mult), reads=[B_init], writes=[B_c2])
            fw.op("pool", lambda h: h.memset(ones_b[:], 1.0), writes=[B_c2])
            fw.op("pool", lambda h: h.memset(sel_b[:], 0.0), writes=[B_sel])
            for v_ in range(2):
                for hm in range(2):
                    fw.op("pool", lambda h, v_=v_, hm=hm: h.tensor_copy(out=sel_b[64 * v_:64 * v_ + 64, v_, 64 * hm:64 * hm + 64],
                                                                     in_=ident_b[64 * v_:64 * v_ + 64, 64 * v_:64 * v_ + 64]),
                          reads=[B_cid], writes=[B_sel])
            mcur_f = mss_f[:, 128:256]
            wst3 = wst_f.rearrange("p (g t) -> p g t", g=8)
            fw.op("dve", lambda h: h.tensor_tensor(out=wst3, in0=wst3, in1=mcur_f.unsqueeze(1).to_broadcast([128, 8, 128]), op=ALU.mult),
                  reads=[], writes=[B_init])
            fw.op("dve", lambda h: h.tensor_copy(out=wst_b[:], in_=wst_f), reads=[B_init], writes=[B_c2])
            for hf in range(2):
                def f(h, hf=hf):
                    return h.matmul(pb[hf][:], lhsT=ones_b[:], rhs=wst_b[:, hf * 512:(hf + 1) * 512], start=True, stop=True)
                fw.op("pe", f, reads=[B_c2], writes=[B_pb[hf]])
                for gg in range(4):
                    g = hf * 4 + gg
                    fw.op("dve", lambda h, g=g, gg=gg, hf=hf: h.scalar_tensor_tensor(
                        out=biasg[:, g * 128:(g + 1) * 128], in0=pb[hf][:, gg * 128:(gg + 1) * 128], scalar=lnb_fm[:, g:g + 1],
                        in1=bs_bc[:, g * 128:(g + 1) * 128], op0=ALU.mult, op1=ALU.add), reads=[B_pb[hf], B_cst, B_init], writes=[B_c2])
            fw.op("act", lambda h: h.activation(out=esink, in_=sink_fm, func=AF.Exp), reads=[B_cst], writes=[B_es])

        B_ycp = [Buf("ycp0"), Buf("ycp1")]

        units = []
        for u in [4, 5, 6, 7, 16, 0, 8, 1, 9, 2, 10, 3, 11, 12, 17, 13, 18, 14, 19, 15, 20]:
            units.append(("in", u * UW))
        for v in range(8):
            units.append(("out", v * UW))
        NU = len(units)

        NHALF = NPASS * NU * 2
        wstate = {"dma": 0, "cast": 0, "allowed": -1}

        def _half_info(hidx):
            gi, hh = hidx // 2, hidx % 2
            kind, c0 = units[gi % NU]
            return gi, hh, kind, c0

        def _emit_dma():
            hidx = wstate["dma"]
            gi, hh, kind, c0 = _half_info(hidx)
            s_ = hidx % NSTG
            wsrc = win_d if kind == "in" else wout_d
            src = wsrc[hh * 1024:(hh + 1) * 1024, c0:c0 + UW].rearrange("(c p) e -> p c e", p=128)
            fw.dma("sp", stg[s_][:], src, f"stg{s_}", writes=[B_stg[s_]])
            wstate["dma"] += 1

        def _emit_cast():
            hidx = wstate["cast"]
            gi, hh, kind, c0 = _half_info(hidx)
            s_ = hidx % NSTG
            slot = gi % NRING
            dst = ring[slot][:, hh * 8:(hh + 1) * 8, :]
            if kind == "in":
                for c in range(8):
                    last = (c == 7)
                    fw.op("pool", lambda h, c=c, dst=dst, s_=s_, hh=hh: h.tensor_scalar(out=dst[:, c, :], in0=stg[s_][:, c, :],
                                                                                      scalar1=gpre_fm[:, 8 * hh + c:8 * hh + c + 1], scalar2=1.0,
                                                                                      op0=ALU.mult, op1=ALU.mult),
                          reads=[B_stg[s_], B_cst], writes=[B_ring[slot]])
            else:
                fw.op("pool", lambda h, dst=dst, s_=s_: h.tensor_scalar(out=dst, in0=stg[s_][:], scalar1=0.5, scalar2=1.0,
                                                                        op0=ALU.mult, op1=ALU.mult), reads=[B_stg[s_]], writes=[B_ring[slot]])
            wstate["cast"] += 1

        def _can_dma():
            return wstate["dma"] < NHALF and wstate["dma"] - wstate["cast"] < NSTG

        def _can_cast():
            return wstate["cast"] < wstate["dma"] and (wstate["cast"] // 2) <= wstate["allowed"]

        def pump(n=1):
            for _ in range(n):
                if _can_cast():
                    _emit_cast()
                if _can_dma():
                    _emit_dma()

        def allow(gi):
            wstate["allowed"] = max(wstate["allowed"], gi)

        def need(gi):
            allow(gi)
            tgt = min(2 * (gi + 1), NHALF)
            while wstate["cast"] < tgt:
                if _can_cast():
                    _emit_cast()
                elif _can_dma():
                    _emit_dma()
                else:
                    raise RuntimeError("weight stream stuck")

        TWO_PI = 2.0 * np.pi
        C1 = 6.28125
        C2 = float(np.float32(TWO_PI - C1))
        C3 = float(TWO_PI - C1 - np.float64(np.float32(TWO_PI - C1)))
        MAGIC = 12582912.0
        PI_S = 3.1415925

        xpref = {"done": False}

        def emit_all():
          if stop == 'init':
            raise _Stop()
          for ps in range(NPASS):
            r0 = ps * PT
            gbase = ps * NU
            scrB = [B_th[0], B_th[1], B_mg[0], B_mg[1], B_gz[0], B_gz[1]]
            xtB = [[B_qf[0], B_qf[1], B_qtr[0], B_qtr[1]],
                   [B_qb[0], B_qb[1], B_ptA[0], B_ptA[1], B_ptB[0], B_ptB[1], B_dn[0], B_dn[1]],
                   [B_th[0], B_th[1], B_mg[0], B_mg[1]],
                   [B_gz[0], B_gz[1]]]

            def a_dma(i, ps_=ps):
                b = i % 4
                extra = []
                rr0 = ps_ * PT
                fw.dma("act", xt[b], x_d[rr0 + 128 * i:rr0 + 128 * (i + 1), :], f"xt{b}", writes=xtB[b] + extra)

            if ps == 0:
                a_dma(0)
                a_dma(1)
                a_dma(2)
                emit_init()
                a_dma(3)
                fw.dma("sp", pos_i, pos_d[0:1, r0:r0 + PTH].partition_broadcast(128), "pos", writes=[B_rope])
                need(gbase + 0)
                xpref["done"] = True
            else:
                fw.dma("sp", pos_i, pos_d[0:1, r0:r0 + PTH].partition_broadcast(128), "pos", writes=[B_rope])
                need(gbase + 1)
            hbB = [B_vn[0:2], B_vn[2:4]]
            sqB = B_vn[4:6]
            rope_ops = []

            def R(en, fn):
                rope_ops.append((en, fn))
            R("dve", lambda h: h.tensor_copy(out=ang, in_=pos_i))
            R("dve", lambda h: h.tensor_scalar(out=ang, in0=ang, scalar1=invf, scalar2=None, op0=ALU.mult))
            R("dve", lambda h: h.tensor_scalar(out=kk, in0=ang, scalar1=float(1.0 / TWO_PI), scalar2=MAGIC, op0=ALU.mult, op1=ALU.add))
            R("dve", lambda h: h.tensor_scalar(out=kk, in0=kk, scalar1=-MAGIC, scalar2=None, op0=ALU.add))
            R("dve", lambda h: h.scalar_tensor_tensor(out=rr, in0=kk, scalar=-C1, in1=ang, op0=ALU.mult, op1=ALU.add))
            R("dve", lambda h: h.scalar_tensor_tensor(out=rr, in0=kk, scalar=-C2, in1=rr, op0=ALU.mult, op1=ALU.add))
            R("dve", lambda h: h.scalar_tensor_tensor(out=rr, in0=kk, scalar=-C3, in1=rr, op0=ALU.mult, op1=ALU.add))
            R("dve", lambda h: h.tensor_scalar(out=kk, in0=rr, scalar1=float(np.pi), scalar2=-TWO_PI, op0=ALU.is_gt, op1=ALU.mult))
            R("dve", lambda h: h.tensor_tensor(out=rr, in0=rr, in1=kk, op=ALU.add))
            R("dve", lambda h: h.tensor_scalar(out=kk, in0=rr, scalar1=float(-np.pi), scalar2=TWO_PI, op0=ALU.is_lt, op1=ALU.mult))
            R("dve", lambda h: h.tensor_tensor(out=rr, in0=rr, in1=kk, op=ALU.add))
            R("dve", lambda h: h.tensor_scalar(out=ang, in0=rr, scalar1=PI_S, scalar2=-PI_S, op0=ALU.min, op1=ALU.max))
            R("dve", lambda h: h.tensor_scalar(out=rr, in0=rr, scalar1=float(np.pi / 2), scalar2=None, op0=ALU.add))
            R("dve", lambda h: h.tensor_scalar(out=kk, in0=rr, scalar1=float(np.pi), scalar2=-TWO_PI, op0=ALU.is_gt, op1=ALU.mult))
            R("dve", lambda h: h.tensor_tensor(out=rr, in0=rr, in1=kk, op=ALU.add))
            R("dve", lambda h: h.tensor_scalar(out=rr, in0=rr, scalar1=PI_S, scalar2=-PI_S, op0=ALU.min, op1=ALU.max))

            def emit_rope(n):
                for _ in range(n):
                    if rope_ops:
                        en, fn = rope_ops.pop(0)
                        fw.op(en, fn, reads=[B_cst], writes=[B_rope] + scrB)

            def a_sq(i):
                b = i % 4
                fw.op("act", lambda h: h.activation(out=sqj, in_=xt[b], func=AF.Square, accum_out=ss0[:, i:i + 1]),
                      reads=xtB[b], writes=sqB + [B_st[i]])
                fw.op("act", lambda h: h.activation(out=ms0[:, i:i + 1], in_=ss0[:, i:i + 1], func=AF.Sqrt, bias=epsc, scale=1.0 / D),
                      reads=[B_eps], writes=[B_st[i]])

            def a_rc(i):
                fw.op("dve", lambda h: h.reciprocal(out=rstd0[:, i:i + 1], in_=ms0[:, i:i + 1]), reads=[], writes=[B_st[i]])

            def a_stt(i):
                b = i % 4
                b2 = i % 2
                fw.op("dve", lambda h: h.tensor_scalar(out=hb[b2], in0=xt[b], scalar1=rstd0[:, i:i + 1], scalar2=None, op0=ALU.mult),
                      reads=xtB[b] + [B_st[i]], writes=hbB[b2])

            def pview(i, g8):
                bank = 2 * (i % 2) + g8
                return bank, pb[bank][:, :].bitcast(BF16).rearrange("p (c t) -> p c t", c=8)

            def b_T(i):
                b2 = i % 2
                for g8 in range(2):
                    bank, pv = pview(i, g8)

                    def f(h, g8=g8, pv=pv):
                        ins = None
                        for c in range(8):
                            cc = g8 * 8 + c
                            ins = h.transpose(pv[:, c, :], hb[b2][:, cc * 128:(cc + 1) * 128], ident_b[:])
                        return ins
                    fw.op("pe", f, reads=hbB[b2] + [B_cid], writes=[B_pb[bank]])

            def b_cp(i):
                for g8 in range(2):
                    bank, pv = pview(i, g8)
                    dst = hT[:, g8 * 8:(g8 + 1) * 8, i * 128:(i + 1) * 128]
                    if g8 == 0:
                        fw.op("act", lambda h, dst=dst, pv=pv: h.copy(out=dst, in_=pv), reads=[B_pb[bank]], writes=[B_hT[i]] + B_yout[0:5])
                    else:
                        fw.op("dve", lambda h, dst=dst, pv=pv: h.tensor_copy(out=dst, in_=pv), reads=[B_pb[bank]], writes=[B_hT[i]] + B_yout[0:5])

            def c_pe(hh, i):
                ua, ub = gbase + 2 * hh, gbase + 2 * hh + 1
                bank = 4 + (i % 2)

                def f(h):
                    ins = None
                    for uu, gu in enumerate((ua, ub)):
                        for c in range(16):
                            ins = h.matmul(pb[bank][:, uu * UW:(uu + 1) * UW], lhsT=hT[:, c, i * 128:(i + 1) * 128],
                                           rhs=ring[gu % NRING][:, c, :], start=(c == 0), stop=(c == 15))
                    return ins
                fw.op("pe", f, reads=[B_hT[i], B_ring[ua % NRING], B_ring[ub % NRING]], writes=[B_pb[bank]])

            def c_post(hh, i):
                bank = 4 + (i % 2)
                vs = B_vst[i - 1]
                fw.op("dve", lambda h: h.bn_stats(out=bst[:, i - 1, hh, :], in_=pb[bank][:]), reads=[B_pb[bank]], writes=[vs])
                if hh == 0:
                    fw.op("act", lambda h: h.copy(out=vraw[:, i - 1, :], in_=pb[bank][:]), reads=[B_pb[bank]], writes=[B_yTa[i - 1]])
                else:
                    fw.op("dve", lambda h: h.bn_aggr(out=mv[:, i - 1, :], in_=bst[:, i - 1, :, :].rearrange("p h s -> p (h s)")),
                          reads=[], writes=[vs])
                    fw.op("act", lambda h: h.activation(out=vr[:, i - 1:i], in_=mv[:, i - 1, 1:2], func=AF.Sqrt, bias=epsc, scale=1.0),
                          reads=[B_eps], writes=[vs])
                    fw.op("dve", lambda h: h.reciprocal(out=rstdv[:, i - 1:i], in_=vr[:, i - 1:i]), reads=[], writes=[vs])
                    fw.op("dve", lambda h: h.scalar_tensor_tensor(out=nmr[:, i - 1:i], in0=mv[:, i - 1, 0:1], scalar=-1.0, in1=rstdv[:, i - 1:i],
                                                                  op0=ALU.mult, op1=ALU.mult), reads=[], writes=[vs])
                    fw.op("act", lambda h: h.activation(out=vn[:, i - 1, 0:512], in_=vraw[:, i - 1, :], func=AF.Identity,
                                                        bias=nmr[:, i - 1:i], scale=rstdv[:, i - 1:i]),
                          reads=[B_yTa[i - 1], vs], writes=[B_vn[i - 1]])
                    fw.op("act", lambda h: h.activation(out=vn[:, i - 1, 512:1024], in_=pb[bank][:], func=AF.Identity,
                                                        bias=nmr[:, i - 1:i], scale=rstdv[:, i - 1:i]),
                          reads=[B_pb[bank], vs], writes=[B_vn[i - 1]])

            if not xpref["done"]:
                for j in range(3):
                    a_dma(j)
            xpref["done"] = False
            for t in range(-1, 12):
                if 4 <= t + 3 <= 8:
                    a_dma(t + 3)
                if t == 2:
                    need(gbase + 1)
                if 0 <= t + 1 <= 8:
                    a_sq(t + 1)
                if t == 3:
                    allow(gbase + 3)
                pump()
                if 0 <= t <= 8:
                    a_stt(t)
                if 0 <= t - 1 <= 8:
                    b_T(t - 1)
                if 1 <= t - 3 <= 8:
                    c_post(0, t - 3)
                if 0 <= t - 1 <= 8:
                    b_cp(t - 1)
                if 0 <= t + 1 <= 8:
                    a_rc(t + 1)
                if 1 <= t - 2 <= 8:
                    c_pe(0, t - 2)

            def phaseA_tile(hh, i):
                c_pe(hh, i)
                c_post(hh, i)

            need(gbase + 3)
            allow(gbase + 5)
            if ps == 0:
                emit_init_late()
            fw.op("pool", lambda h: h.memset(arena[:, o2:o3], 1.0), writes=[B_VAB, B_init] + B_yout[5:7])
            for i in range(1, 9):
                phaseA_tile(1, i)
                emit_rope(2)
                pump()
            emit_rope(99)
            fw.op("act", lambda h: h.activation(out=sinT, in_=ang, func=AF.Sin, scale=sgn), reads=[B_cst] + scrB, writes=[B_rope] + B_yout[6:8])
            fw.op("act", lambda h: h.activation(out=cosT, in_=rr, func=AF.Sin), reads=scrB, writes=[B_rope] + B_yout[6:8])
            if debug and ps == 0:
                fw.dma("sp", dbg["dbg_hT"], arena[:, 0:16 * PTH], "dbg", reads=B_hT)
                fw.dma("sp", dbg["dbg_cs"], arena[:, o3:o5].bitcast(F32), "dbg", reads=[B_rope])
            if stop == 'A' and ps == 0:
                raise _Stop()
            ukv = gbase + 4
            need(ukv)
            allow(ukv + 3)
            def v_tile(i):
                bank = i % 2

                def f(h, i=i, bank=bank, ukv=ukv):
                    ins = None
                    for c in range(16):
                        ins = h.matmul(pb[bank][:, 0:128], lhsT=hT[:, c, i * 128:(i + 1) * 128], rhs=ring[ukv % NRING][:, c, 128:256],
                                       start=(c == 0), stop=(c == 15))
                    return ins
                fw.op("pe", f, reads=B_hT + [B_ring[ukv % NRING]], writes=[B_pb[bank]])
                pv3 = pb[bank][:, 0:128].rearrange("p (k d) -> p k d", k=2)
                bv3 = bv_bc[:, :].rearrange("p (k d) -> p k d", k=2)
                fw.op("dve", lambda h, i=i, pv3=pv3, bv3=bv3: h.tensor_tensor(out=VAB[:, i, :, 0, 0:64], in0=pv3, in1=bv3, op=ALU.add),
                      reads=[B_pb[bank], B_cst], writes=[B_VAB])
                fw.op("dve", lambda h, i=i, pv3=pv3, bv3=bv3: h.tensor_tensor(out=VAB[:, i, :, 1, 64:128], in0=pv3, in1=bv3, op=ALU.add),
                      reads=[B_pb[bank], B_cst], writes=[B_VAB])
                pump()
            slabs3 = [(0, 512), (512, 512), (1024, 128)]
            ktmp = qtr

            def k1(s_):
                t0, n = slabs3[s_]
                b = s_ % 2
                bank = (2, 5)[b]

                def f(h, ukv=ukv):
                    ins = None
                    for c in range(16):
                        ins = h.matmul(pb[bank][:, 0:n], lhsT=ring[ukv % NRING][:, c, 0:128], rhs=hT[:, c, t0:t0 + n],
                                       start=(c == 0), stop=(c == 15))
                    return ins
                fw.op("pe", f, reads=B_hT + [B_ring[ukv % NRING]], writes=[B_pb[bank]])
                fw.op("act", lambda h: h.activation(out=qf[b][:, 0:n], in_=pb[bank][:, 0:n], func=AF.Identity, bias=bk_nat),
                      reads=[B_pb[bank], B_cst], writes=[B_kf[b]])
                fw.op("act", lambda h: h.activation(out=qb[b][:, 0:n], in_=pb[bank][:, 0:n], func=AF.Identity, bias=bk_nat),
                      reads=[B_pb[bank], B_cst], writes=[B_kb[b]])

            def k2(s_):
                t0, n = slabs3[s_]
                b = s_ % 2
                fw.op("pe", lambda h: h.matmul(pb[6][:, 0:n], lhsT=pswap_b[:], rhs=qb[b][:, 0:n], start=True, stop=True),
                      reads=[B_kb[b], B_c2], writes=[B_pb[6]])
                fw.op("dve", lambda h: h.tensor_tensor(out=t1[b][:, 0:n], in0=qf[b][:, 0:n], in1=cosT[:, t0:t0 + n], op=ALU.mult),
                      reads=[B_rope], writes=[B_t1[b]])
                fw.op("dve", lambda h: h.tensor_tensor(out=t2[b][:, 0:n], in0=pb[6][:, 0:n], in1=sinT[:, t0:t0 + n], op=ALU.mult),
                      reads=[B_pb[6], B_rope], writes=[B_t2[b]])
                fw.op("dve", lambda h: h.tensor_tensor(out=ktmp[b][:, 0:n], in0=t1[b][:, 0:n], in1=t2[b][:, 0:n], op=ALU.add),
                      reads=[B_t1[b], B_t2[b]], writes=[B_qtr[b]])

            def k3(s_):
                t0, n = slabs3[s_]
                b = s_ % 2
                for kv in range(2):
                    bk_ = 7 if kv == 0 else 3
                    fw.op("pe", lambda h, kv=kv, bk_=bk_: h.matmul(pb[bk_][:, 0:n], lhsT=sel_b[:, kv, :], rhs=ktmp[b][:, 0:n], start=True, stop=True),
                          reads=[B_qtr[b], B_sel], writes=[B_pb[bk_]])
                    if kv == 0:
                        fw.op("act", lambda h, kv=kv, bk_=bk_: h.copy(out=KT[:, kv, t0:t0 + n], in_=pb[bk_][:, 0:n]),
                              reads=[B_pb[bk_]], writes=[B_KT, B_init] + B_yout[4:6])
                    else:
                        fw.op("dve", lambda h, kv=kv, bk_=bk_: h.tensor_copy(out=KT[:, kv, t0:t0 + n], in_=pb[bk_][:, 0:n]),
                              reads=[B_pb[bk_]], writes=[B_KT, B_init] + B_yout[4:6])

            kjobs = slabs3
            vq = list(range(9))
            for s_ in range(len(kjobs) + 2):
                if s_ < len(kjobs):
                    k1(s_)
                if vq:
                    v_tile(vq.pop(0))
                if 1 <= s_ <= len(kjobs):
                    k2(s_ - 1)
                if vq:
                    v_tile(vq.pop(0))
                if s_ >= 2:
                    k3(s_ - 2)
                pump()
            while vq:
                v_tile(vq.pop(0))
            if debug and ps == 0:
                fw.dma("sp", dbg["dbg_vn"], vn_t[:, :], "dbg", reads=B_vn)
                fw.dma("sp", dbg["dbg_KT"], arena[:, o1:o2], "dbg", reads=[B_KT])

            if stop == 'A3' and ps == 0:
                raise _Stop()
            for q2 in range(2):
                rows = ps * PT + q2 * 512
                fw.dma("pool", y_d[rows:rows + 512, :], x_d[HALO + rows:HALO + rows + 512, :], f"ycp{q2}", writes=[B_ycp[q2]])
            if ps == 0:
                fw.dma("pool", gbc[:], gpost_d.partition_broadcast(128), "gbc", writes=[B_gbc])
            itb = 0
            for j in range(4):
                uu_, uz_ = gbase + 5 + 2 * j, gbase + 6 + 2 * j
                need(uz_)
                allow(uz_ + 2)
                su, sz = uu_ % NRING, uz_ % NRING
                for gg in range(2):
                    g = 2 * j + gg
                    for sl in range(2):
                        t0 = HALO + 512 * sl
                        b = itb % 2
                        itb += 1
                        bU, bZ, bM = b, 2 + b, 4 + b

                        def fu(h, su=su, gg=gg, t0=t0, bU=bU):
                            ins = None
                            for c in range(16):
                                ins = h.matmul(pb[bU][:], lhsT=ring[su][:, c, gg * 128:(gg + 1) * 128], rhs=hT[:, c, t0:t0 + 512],
                                               start=(c == 0), stop=(c == 15))
                            return ins
                        fw.op("pe", fu, reads=B_hT + [B_ring[su]], writes=[B_pb[bU]])

                        def fz(h, sz=sz, gg=gg, t0=t0, bZ=bZ):
                            ins = None
                            for c in range(16):
                                ins = h.matmul(pb[bZ][:], lhsT=ring[sz][:, c, gg * 128:(gg + 1) * 128], rhs=hT[:, c, t0:t0 + 512],
                                               start=(c == 0), stop=(c == 15))
                            return ins
                        fw.op("pe", fz, reads=B_hT + [B_ring[sz]], writes=[B_pb[bZ]])

                        def fm(h, g=g, sl=sl, bM=bM):
                            ins = None
                            for ch in range(4):
                                ins = h.matmul(pb[bM][:, ch * 128:(ch + 1) * 128], lhsT=vn[:, 4 * sl + ch, g * 128:(g + 1) * 128],
                                               rhs=wst_b[:, g * 128:(g + 1) * 128], start=True, stop=True)
                            return ins
                        fw.op("pe", fm, reads=[B_vn[4 * sl + c_] for c_ in range(4)] + [B_c2], writes=[B_pb[bM]])
                        fw.op("act", lambda h, b=b, bZ=bZ: h.activation(out=th[b], in_=pb[bZ][:], func=AF.Tanh, scale=0.5),
                              reads=[B_pb[bZ]], writes=[B_th[b]])
                        fw.op("dve", lambda h, b=b, bZ=bZ: h.scalar_tensor_tensor(out=az[b], in0=th[b], scalar=1.0, in1=pb[bZ][:],
                                                                                 op0=ALU.add, op1=ALU.mult), reads=[B_th[b], B_pb[bZ]], writes=[B_az[b]])
                        fw.op("dve", lambda h, b=b, bM=bM, g=g: h.scalar_tensor_tensor(
                            out=mg[b].rearrange("p (c t) -> p c t", c=4), in0=pb[bM][:, :].rearrange("p (c t) -> p c t", c=4),
                            scalar=lng_fm[:, g:g + 1], in1=biasg[:, g * 128:(g + 1) * 128].unsqueeze(1).to_broadcast([128, 4, 128]),
                            op0=ALU.mult, op1=ALU.add), reads=[B_pb[bM], B_c2, B_cst], writes=[B_mg[b]])
                        fw.op("dve", lambda h, b=b, bU=bU: h.tensor_tensor(out=mg[b], in0=mg[b], in1=pb[bU][:], op=ALU.mult),
                              reads=[B_mg[b], B_pb[bU]], writes=[B_mg[b]])
                        fw.op("dve", lambda h, b=b, g=g, sl=sl: h.tensor_tensor(out=yTa[:, g, sl * 512:(sl + 1) * 512], in0=mg[b], in1=az[b], op=ALU.mult),
                              reads=[B_mg[b], B_az[b]], writes=[B_yTa[g]])
                        pump()
            if debug and ps == 0:
                fw.dma("sp", dbg["dbg_yTa"], yTa_t[:, :], "dbg", reads=B_yTa)

            if stop == 'B' and ps == 0:
                raise _Stop()
            def proj_steps(ch, gbase=gbase):
                j, cc = ch // 2, ch % 2
                uq_, uzb_ = gbase + 13 + 2 * j, gbase + 14 + 2 * j
                sq, szb = uq_ % NRING, uzb_ % NRING
                cb = ch % 2

                def q_part(sl):
                    t0 = HALO + 512 * sl
                    bq_ = 0 if sl == 0 else 7
                    if cc == 0 and sl == 0:
                        need(uzb_)
                        allow(uzb_ + 2)

                    def fq(h):
                        ins = None
                        for c in range(16):
                            ins = h.matmul(pb[bq_][:], lhsT=ring[sq][:, c, cc * 128:(cc + 1) * 128], rhs=hT[:, c, t0:t0 + 512],
                                           start=(c == 0), stop=(c == 15))
                        return ins
                    fw.op("pe", fq, reads=B_hT + [B_ring[sq]], writes=[B_pb[bq_]])
                    fw.op("act", lambda h: h.activation(out=qf[sl][:], in_=pb[bq_][:], func=AF.Identity, bias=bq_fm[:, ch:ch + 1]),
                          reads=[B_pb[bq_], B_cst], writes=[B_qf[sl]])
                    fw.op("act", lambda h: h.activation(out=qb[sl][:], in_=pb[bq_][:], func=AF.Identity, bias=bq_fm[:, ch:ch + 1]),
                          reads=[B_pb[bq_], B_cst], writes=[B_qb[sl]])

                def rot_part(sl):
                    t0 = HALO + 512 * sl
                    fw.op("pe", lambda h: h.matmul(pb[1][:], lhsT=pswap_b[:], rhs=qb[sl][:], start=True, stop=True),
                          reads=[B_qb[sl], B_c2], writes=[B_pb[1]])
                    fw.op("pool", lambda h: h.tensor_tensor(out=qf[sl][:], in0=qf[sl][:], in1=cosT[:, t0:t0 + 512], op=ALU.mult),
                          reads=[B_rope], writes=[B_qf[sl]])
                    fw.op("dve", lambda h: h.tensor_tensor(out=t2[sl], in0=pb[1][:], in1=sinT[:, t0:t0 + 512], op=ALU.mult),
                          reads=[B_pb[1], B_rope], writes=[B_t2[sl]])
                    fw.op("dve", lambda h: h.tensor_tensor(out=qtr[cb][:, sl * 512:(sl + 1) * 512], in0=qf[sl][:], in1=t2[sl], op=ALU.add),
                          reads=[B_qf[sl], B_t2[sl]], writes=[B_qtr[cb]])

                def z_part(sl):
                    t0 = HALO + 512 * sl

                    def fzb(h):
                        ins = None
                        for c in range(16):
                            ins = h.matmul(pb[2][:], lhsT=ring[szb][:, c, cc * 128:(cc + 1) * 128], rhs=hT[:, c, t0:t0 + 512],
                                           start=(c == 0), stop=(c == 15))
                        return ins
                    fw.op("pe", fzb, reads=B_hT + [B_ring[szb]], writes=[B_pb[2]])
                    fw.op("act", lambda h: h.activation(out=th[sl], in_=pb[2][:], func=AF.Tanh, scale=0.5), reads=[B_pb[2]], writes=[B_th[sl]])
                    fw.op("dve", lambda h: h.scalar_tensor_tensor(out=gz[cb][:, sl * 512:(sl + 1) * 512], in0=th[sl], scalar=1.0, in1=pb[2][:],
                                                                  op0=ALU.add, op1=ALU.mult), reads=[B_th[sl], B_pb[2]], writes=[B_gz[cb]])

                def s0():
                    q_part(0)

                def s1():
                    q_part(1)
                    rot_part(0)

                def s2():
                    z_part(0)
                    rot_part(1)

                def s3():
                    z_part(1)
                return [s0, s1, s2, s3]

            def attn_steps(ch, ps=ps):
                kv = ch // 4
                cb = ch % 2
                steps = []
                for bp in range(4):
                    pbuf = bp % 2
                    bSA, bSB, bO = 3, 4, 5 + pbuf
                    mask = mfs_b if (ps == 0 and bp == 0) else mss_b

                    def step_s(kv=kv, cb=cb, bp=bp, pbuf=pbuf, bSA=bSA, bSB=bSB, mask=mask):
                        def fs(h):
                            ins = None
                            for (r0_, bank) in ((0, bSA), (64, bSB)):
                                h.matmul(pb[bank][:], lhsT=ident_b[:], rhs=mask[:], start=True, stop=False)
                                for blk in range(2):
                                    bl = 2 * bp + blk
                                    for kc in range(2):
                                        ktile = bl + kc
                                        col = (blk * 2 + kc) * 128
                                        ins = h.matmul(pb[bank][:, col:col + 128], lhsT=KT[r0_:r0_ + 64, kv, ktile * 128:(ktile + 1) * 128],
                                                       rhs=qtr[cb][r0_:r0_ + 64, bl * 128:(bl + 1) * 128], start=False,
                                                       stop=(blk == 1 and kc == 1))
                            return ins
                        fw.op("pe", fs, reads=[B_KT, B_qtr[cb], B_c2, B_cid], writes=[B_pb[bSA], B_pb[bSB]])
                        fw.op("act", lambda h: h.activation(out=ptA[pbuf][:], in_=pb[bSA][:], func=AF.Exp, scale=0.125),
                              reads=[B_pb[bSA]], writes=[B_ptA[pbuf]])
                        fw.op("act", lambda h: h.activation(out=ptB[pbuf][:], in_=pb[bSB][:], func=AF.Exp, scale=0.125),
                              reads=[B_pb[bSB]], writes=[B_ptB[pbuf]])

                    def step_pv(kv=kv, cb=cb, bp=bp, pbuf=pbuf, bO=bO, ch=ch):
                        def fpv(h):
                            ins = None
                            for (a_, pt) in ((0, ptA), (1, ptB)):
                                for blk in range(2):
                                    bl = 2 * bp + blk
                                    for kc in range(2):
                                        vt = bl + kc
                                        col = (blk * 2 + kc) * 128
                                        ins = h.matmul(pb[bO][:, a_ * 256 + blk * 128:a_ * 256 + (blk + 1) * 128], lhsT=VAB[:, vt, kv, a_, :],
                                                       rhs=pt[pbuf][:, col:col + 128], start=(kc == 0), stop=(kc == 1))
                            return ins
                        fw.op("pe", fpv, reads=[B_VAB, B_ptA[pbuf], B_ptB[pbuf]], writes=[B_pb[bO]])
                        fw.op("act", lambda h: h.activation(out=dn[pbuf][64:128, :], in_=pb[bO][64:128, 0:256], func=AF.Identity, bias=esink[64:128, ch:ch + 1]),
                              reads=[B_pb[bO], B_es], writes=[B_dn[pbuf]])
                        fw.op("act", lambda h: h.activation(out=dn[pbuf][0:64, :], in_=pb[bO][0:64, 256:512], func=AF.Identity, bias=esink[0:64, ch:ch + 1]),
                              reads=[B_pb[bO], B_es], writes=[B_dn[pbuf]])
                        fw.op("dve", lambda h: h.reciprocal(out=rd[pbuf][:], in_=dn[pbuf][:]), reads=[], writes=[B_dn[pbuf]])
                        fw.op("dve", lambda h: h.tensor_tensor(out=yn[pbuf][0:64, :], in0=pb[bO][0:64, 0:256], in1=rd[pbuf][64:128, :], op=ALU.mult),
                              reads=[B_pb[bO], B_rd[pbuf]], writes=[B_yn[pbuf]])
                        fw.op("dve", lambda h: h.tensor_tensor(out=yn[pbuf][64:128, :], in0=pb[bO][64:128, 256:512], in1=rd[pbuf][0:64, :], op=ALU.mult),
                              reads=[B_pb[bO], B_rd[pbuf]], writes=[B_yn[pbuf]])
                        fw.op("dve", lambda h: h.tensor_tensor(out=yTb[:, ch, bp * 256:(bp + 1) * 256], in0=yn[pbuf][:],
                                                               in1=gz[cb][:, bp * 256:(bp + 1) * 256], op=ALU.mult),
                              reads=[B_yn[pbuf], B_gz[cb]], writes=[B_yTb[ch]])
                    steps += [step_s, step_pv]
                return steps

            dfill = {}

            def fill_steps(gbase=gbase):
                uo = gbase + 21
                so = uo % NRING
                steps = []
                for k_, bank in enumerate((0, 1, 2, 7, 3, 4)):
                    def st(k_=k_, bank=bank):
                        if k_ == 0:
                            need(uo)

                        def fo(h):
                            ins = None
                            for c in range(15):
                                src = yTa if c < 8 else yTb
                                ins = h.matmul(pb[bank][:, 0:UW], lhsT=src[:, c % 8, k_ * 128:(k_ + 1) * 128], rhs=ring[so][:, c, :],
                                               start=(c == 0), stop=False)
                            return ins
                        fw.op("pe", fo, reads=[B_ring[so]] + B_yTa + B_yTb[0:7], writes=[B_pb[bank]])
                        dfill[k_] = bank
                    steps.append(st)
                return steps

            for st_ in proj_steps(0):
                st_()
                pump()
            for ch in range(8):
                A_ = attn_steps(ch)
                P_ = proj_steps(ch + 1) if ch + 1 < 8 else fill_steps()
                if ch + 1 < 8:
                    order = [A_[0]] + P_[0:1] + [A_[1], A_[2]] + P_[1:2] + [A_[3], A_[4]] + P_[2:3] + [A_[5], A_[6]] + P_[3:4] + [A_[7]]
                else:
                    order = [A_[0], A_[1], A_[2]] + P_[0:1] + [A_[3], A_[4]] + P_[1:2] + [A_[5], A_[6]] + P_[2:3] + [A_[7]] + P_[3:6]
                for st_ in order:
                    st_()
                    pump()
            if debug and ps == 0:
                fw.dma("sp", dbg["dbg_yTb"], vn_t[:, :], "dbg", reads=B_yTb)

            if stop == 'C' and ps == 0:
                raise _Stop()
            nonlocal_itd = [0]
            arenaB = {0: B_hT, 1: B_hT, 2: B_hT, 3: B_hT, 4: B_hT + [B_KT], 5: [B_KT, B_VAB], 6: [B_VAB, B_rope], 7: [B_rope]}
            if ps + 1 < NPASS:
                for j in range(4):
                    a_dma(j, ps + 1)
                xpref["done"] = True
            B_sq8 = B_ssq8

            def d_group(v, i):
                nonlocal_itd[0] += 1
                bank = nonlocal_itd[0] % 4
                uo = gbase + 21 + v
                so = uo % NRING

                pre = (v == 0 and i in dfill)
                if pre:
                    bank = dfill[i]

                def fo(h):
                    ins = None
                    for c in (range(15, 16) if pre else range(16)):
                        src = yTa if c < 8 else yTb
                        ins = h.matmul(pb[bank][:, 0:UW], lhsT=src[:, c % 8, i * 128:(i + 1) * 128], rhs=ring[so][:, c, :],
                                       start=(c == 0), stop=(c == 15))
                    return ins
                fw.op("pe", fo, reads=[B_ring[so]] + B_yTa + B_yTb, writes=[B_pb[bank]])
                fw.op("act", lambda h: h.activation(out=sqj2[:], in_=pb[bank][:, 0:UW], func=AF.Square, accum_out=ssq[:, i, v:v + 1]),
                      reads=[B_pb[bank]], writes=[B_sqj2, B_sq8[i]])
                fw.op("dve", lambda h: h.tensor_tensor(out=yout[:, i, v * UW:(v + 1) * UW], in0=pb[bank][:, 0:UW],
                                                       in1=gbc[:, v * UW:(v + 1) * UW], op=ALU.mult),
                      reads=[B_pb[bank], B_gbc], writes=[B_yout[i]] + arenaB[i])
                if nonlocal_itd[0] % 2 == 0:
                    pump()

            def d_final(i):
                Bq = B_sq8[i]
                fw.op("dve", lambda h: h.tensor_reduce(out=ssd[:, i:i + 1], in_=ssq[:, i, :], axis=AX.X, op=ALU.add), reads=[], writes=[Bq])
                fw.op("act", lambda h: h.activation(out=msd[:, i:i + 1], in_=ssd[:, i:i + 1], func=AF.Sqrt, bias=epsc, scale=1.0 / D), reads=[B_eps], writes=[Bq])
                fw.op("dve", lambda h: h.reciprocal(out=rstd2[:, i:i + 1], in_=msd[:, i:i + 1]), reads=[], writes=[Bq])
                if i % 2 == 0:
                    fw.op("dve", lambda h: h.tensor_scalar(out=yout[:, i, :], in0=yout[:, i, :], scalar1=rstd2[:, i:i + 1], scalar2=None, op0=ALU.mult),
                          reads=[Bq], writes=[B_yout[i]])
                else:
                    fw.op("act", lambda h: h.activation(out=yout[:, i, :], in_=yout[:, i, :], func=AF.Copy, scale=rstd2[:, i:i + 1]),
                          reads=[Bq], writes=[B_yout[i]])
                orow = ps * PT + 128 * i
                eng = fw.E["pool"]
                wl = fw._emit_waits(eng, fw._deps([B_yout[i]] + B_ycp, [], []))
                sem, cnt = fw.dma_sems[f"st{i}"]
                cnt += 16
                fw.dma_sems[f"st{i}"] = (sem, cnt)

                def run(h, wl=wl, sem=sem, orow=orow):
                    for (s_, v_) in wl:
                        h.wait_ge(s_, v_)
                    h.dma_start(out=y_d[orow:orow + 128, :], in_=yout[:, i, :], accum_op=ALU.add).then_inc(sem, 16)
                eng.ops.append(run)
                B_yout[i].r.append((sem, cnt, f"dma:st{i}"))

            for v in range(6):
                uo = gbase + 21 + v
                need(uo)
                allow(uo + 3)
                for i in range(8):
                    d_group(v, i)
            need(gbase + 21 + 7)
            allow(gbase + 21 + 7 + 2)
            for i in range(8):
                d_group(6, i)
                d_group(7, i)
                if i >= 1:
                    d_final(i - 1)
            d_final(7)
          fw.fence()

        try:
            emit_all()
        except _Stop:
            fw.fence()
        with nc.Block() as block:
            fw.replay(block)
    return nc


_CACHE = {}


def _host_consts():
    k = np.arange(128)[:, None]
    q = np.arange(128)[None, :]
    mprev = (k > q).astype(np.float32)
    mcur = (k <= q).astype(np.float32)
    zero = np.zeros_like(mprev)
    mask_ss = np.concatenate([mprev, mcur, mprev, mcur], axis=1)
    mask_zs = np.concatenate([zero, mcur, mprev, mcur], axis=1)
    ident = np.eye(128, dtype=np.float32)
    m = np.arange(128)
    sw = np.where((m % 64) < 32, m + 32, m - 32)
    pswap = np.zeros((128, 128), np.float32)
    pswap[sw, m] = 1.0
    half = 32
    inv_freq = (np.float32(10000.0) ** (-np.arange(half, dtype=np.float32) * np.float32(2.0 / 64))).astype(np.float32)
    invf = inv_freq[m % 32].reshape(128, 1).astype(np.float32)
    sgn = np.where((m % 64) < 32, -1.0, 1.0).astype(np.float32).reshape(128, 1)
    return dict(mask_ss=mask_ss, mask_zs=mask_zs, ident=ident, pswap=pswap, invf=invf, sgn=sgn)


def make_in_maps(x, positions, g_pre, w_in, b_qkv, ln_v_g, ln_v_b, w_spatial, b_spatial, attn_sinks, w_out, g_post):
    hc = _host_consts()
    f32 = np.float32
    x = np.asarray(x, f32)
    positions = np.asarray(positions, np.int32)
    w_in0 = np.ascontiguousarray(np.asarray(w_in, f32)[0])
    w_out0 = np.ascontiguousarray(np.asarray(w_out, f32)[0])
    bqkv = np.asarray(b_qkv, f32)[0]
    bq_fm = np.ascontiguousarray(bqkv[0:1024].reshape(8, 128).T)
    bk = bqkv[1024:1152]
    bk_fm = np.ascontiguousarray(np.stack([np.tile(bk[0:64], 2), np.tile(bk[64:128], 2)], axis=1))
    bv = np.ascontiguousarray(bqkv[1152:1280].reshape(1, 128))
    lng_fm = np.ascontiguousarray(np.asarray(ln_v_g, f32)[0].reshape(8, 128).T)
    lnb_fm = np.ascontiguousarray(np.asarray(ln_v_b, f32)[0].reshape(8, 128).T)
    ws = np.asarray(w_spatial, f32)[0]
    wsT = np.ascontiguousarray(np.transpose(ws, (2, 0, 1)).reshape(128, 1024))
    bs = np.ascontiguousarray(np.asarray(b_spatial, f32)[0].reshape(1, 1024))
    sinks = np.asarray(attn_sinks, f32)[0]
    sink_fm = np.zeros((128, 8), f32)
    for ch in range(8):
        sink_fm[0:64, ch] = sinks[2 * ch + 1]
        sink_fm[64:128, ch] = sinks[2 * ch]
    bs_bc = np.broadcast_to(bs, (128, 1024))
    bv_bc = np.broadcast_to(bv, (128, 128))
    gpre_fm = np.asarray(g_pre, f32)[0].reshape(16, 128).T
    cpa = np.ascontiguousarray(np.concatenate([bv_bc, bq_fm, bk_fm, lng_fm, lnb_fm, sink_fm, hc["invf"], hc["sgn"], gpre_fm, bk.reshape(128, 1)], axis=1), dtype=f32)
    cpb_s = np.concatenate([hc["ident"], hc["pswap"], hc["mask_ss"], hc["mask_ss"], wsT, bs_bc], axis=1).astype(f32)
    cpb_z = np.concatenate([hc["ident"], hc["pswap"], hc["mask_ss"], hc["mask_zs"], wsT, bs_bc], axis=1).astype(f32)
    common = dict(w_in=w_in0, w_out=w_out0, g_pre=np.asarray(g_pre, f32)[0].reshape(1, D).copy(),
                  g_post=np.asarray(g_post, f32)[0].reshape(1, D).copy(), cpa=cpa)
    in_maps = []
    for c in range(8):
        b, hf = c // 2, c % 2
        s0 = hf * NTOK
        xs = np.zeros((NTOK + HALO, D), f32)
        xs[HALO:] = x[b, s0:s0 + NTOK]
        ps_ = np.zeros((1, NTOK + HALO), np.int32)
        ps_[0, HALO:] = positions[b, s0:s0 + NTOK]
        if hf == 1:
            xs[:HALO] = x[b, s0 - HALO:s0]
            ps_[0, :HALO] = positions[b, s0 - HALO:s0]
        m = dict(common)
        m["x"] = xs
        m["pos"] = ps_
        m["cpb"] = cpb_s if hf == 1 else cpb_z
        in_maps.append(m)
    return in_maps


def kernel(x, positions, g_pre, w_in, b_qkv, ln_v_g, ln_v_b, w_spatial, b_spatial, attn_sinks, w_out, g_post):
    if "nc" not in _CACHE:
        _CACHE["nc"] = build_program(False)
    nc = _CACHE["nc"]
    in_maps = make_in_maps(x, positions, g_pre, w_in, b_qkv, ln_v_g, ln_v_b, w_spatial, b_spatial, attn_sinks, w_out, g_post)
    res = run_bass_kernel_spmd(nc, in_maps, core_ids=list(range(8)))
    out = np.empty((4, 4096, D), np.float32)
    for c in range(8):
        b, hf = c // 2, c % 2
        out[b, hf * NTOK:(hf + 1) * NTOK] = res.results[c]["y"]
    return out
```

```python
import numpy as np
from contextlib import ExitStack

import concourse.bass as bass
import concourse.mybir as mybir
from concourse.bass_utils import run_bass_kernel_spmd

F32 = mybir.dt.float32
BF16 = mybir.dt.bfloat16
I32 = mybir.dt.int32
ALU = mybir.AluOpType
AF = mybir.ActivationFunctionType
AX = mybir.AxisListType

D = 2048
NTOK = 2048
HALO = 128
NPASS = 2
PT = 1024
PTH = PT + HALO
NT = PT // 128
E_IN = 5376
EPS = 1e-6
UW = 256
NRING = 4
NSTG = 2
MASKNEG = 2000.0


class Eng:
    def __init__(self, name, sem):
        self.name = name
        self.sem = sem
        self.count = 0
        self.waited = {}
        self.ops = []


class Buf:
    def __init__(self, name="", excl=False):
        self.name = name
        self.w = None
        self.r = []
        self.excl = excl


class FW:
    def __init__(self, nc, sems):
        self.nc = nc
        self.E = {n: Eng(n, sems[n]) for n in ("pe", "act", "dve", "pool", "sp")}
        self.dma_sems = {}

    def _emit_waits(self, eng, waits):
        need = {}
        for ev in waits:
            if ev is None:
                continue
            sem, val, key = ev
            if eng.waited.get(key, 0) >= val:
                continue
            if need.get(key, (None, 0))[1] < val:
                need[key] = (sem, val)
        out = []
        for key, (sem, val) in need.items():
            eng.waited[key] = val
            out.append((sem, val))
        return out

    def _deps(self, reads, writes, waits):
        evs = list(waits)
        for b in reads:
            if b.w is not None:
                evs.append(b.w)
            if b.excl:
                evs.extend(b.r)
        for b in writes:
            evs.extend(b.r)
            if b.w is not None:
                evs.append(b.w)
        return evs

    def _commit(self, ev, reads, writes):
        for b in reads:
            if b.excl:
                b.w = ev
                b.r = []
            else:
                b.r.append(ev)
        for b in writes:
            b.w = ev
            b.r = []

    def op(self, en, fn, reads=(), writes=(), waits=()):
        eng = self.E[en]
        evs = self._deps(reads, writes, waits)
        if en == "pe":
            evs = [e for e in evs if e is not None and e[2] != "pe"]
        wl = self._emit_waits(eng, evs)
        eng.count += 1
        cnt = eng.count
        sem = eng.sem

        def run(h, fn=fn, wl=wl, sem=sem):
            for (s, v) in wl:
                h.wait_ge(s, v)
            ins = fn(h)
            ins.then_inc(sem, 1)

        eng.ops.append(run)
        ev = (sem, cnt, en)
        self._commit(ev, reads, writes)
        return ev

    def dma(self, en, out, in_, sem_name, reads=(), writes=(), waits=()):
        eng = self.E[en]
        evs = self._deps(reads, writes, waits)
        wl = self._emit_waits(eng, evs)
        sem, cnt = self.dma_sems[sem_name]
        cnt += 16
        self.dma_sems[sem_name] = (sem, cnt)

        def run(h, wl=wl, sem=sem, out=out, in_=in_):
            for (s, v) in wl:
                h.wait_ge(s, v)
            h.dma_start(out=out, in_=in_).then_inc(sem, 16)

        eng.ops.append(run)
        ev = (sem, cnt, "dma:" + sem_name)
        self._commit(ev, reads, writes)
        return ev

    def all_events(self):
        evs = []
        for n, e in self.E.items():
            if e.count > 0:
                evs.append((e.sem, e.count, n))
        for n, (sem, cnt) in self.dma_sems.items():
            if cnt > 0:
                evs.append((sem, cnt, "dma:" + n))
        return evs

    def fence(self, engines=("pe", "act", "dve", "pool", "sp")):
        evs = self.all_events()
        for n in engines:
            eng = self.E[n]
            wl = self._emit_waits(eng, [e for e in evs if e[2] != n])

            def run(h, wl=wl):
                for (s, v) in wl:
                    h.wait_ge(s, v)

            eng.ops.append(run)

    def replay(self, block):
        E = self.E

        @block.sync
        def _(h):
            for f in E["sp"].ops:
                f(h)

        @block.tensor
        def _(h):
            for f in E["pe"].ops:
                f(h)

        @block.scalar
        def _(h):
            for f in E["act"].ops:
                f(h)

        @block.vector
        def _(h):
            for f in E["dve"].ops:
                f(h)

        @block.gpsimd
        def _(h):
            for f in E["pool"].ops:
                f(h)


class _Stop(Exception):
    pass


def build_program(debug=False, stop=None):
    nc = bass.Bass("TRN2", target_bir_lowering=False)

    def din(name, shape, dt=F32):
        return nc.dram_tensor(name, list(shape), dt, kind="ExternalInput").ap()

    x_d = din("x", [NTOK + HALO, D])
    pos_d = din("pos", [1, NTOK + HALO], I32)
    win_d = din("w_in", [D, E_IN])
    wout_d = din("w_out", [D, D])
    gpre_d = din("g_pre", [1, D])
    gpost_d = din("g_post", [1, D])
    cpa_d = din("cpa", [128, 181])
    cpb_d = din("cpb", [128, 3328])
    y_d = nc.dram_tensor("y", [NTOK, D], F32, kind="ExternalOutput").ap()
    dbg = {}
    if debug:
        for nm, shp, dt in [("dbg_hT", [128, 16 * PTH], BF16), ("dbg_vn", [128, 8 * 1024], BF16),
                            ("dbg_KT", [128, 2 * PTH], BF16), ("dbg_yTa", [128, 8 * 1024], BF16),
                            ("dbg_yTb", [128, 8 * 1024], BF16), ("dbg_cs", [128, 2 * PTH], F32)]:
            dbg[nm] = nc.dram_tensor(nm, shp, dt, kind="ExternalOutput").ap()

    with ExitStack() as es:
        def sb(name, shape, dt):
            return es.enter_context(nc.sbuf_tensor(name, list(shape), dt))

        sems = {n: es.enter_context(nc.semaphore("s_" + n)) for n in ("pe", "act", "dve", "pool", "sp")}
        fw = FW(nc, sems)
        for n in ["cst", "cst2", "pos", "gbc", "xt0", "xt1", "xt2", "xt3", "ycp0", "ycp1", "dbg"] + [f"st{i}" for i in range(8)] + [f"stg{i}" for i in range(NSTG)]:
            fw.dma_sems[n] = (es.enter_context(nc.semaphore("d_" + n)), 0)

        arena = sb("arena", [128, 32768], BF16)
        hT = arena[:, 0:16 * PTH].rearrange("p (c t) -> p c t", c=16)
        o1 = 16 * PTH
        KT = arena[:, o1:o1 + 2 * PTH].rearrange("p (k t) -> p k t", k=2)
        o2 = o1 + 2 * PTH
        VAB = arena[:, o2:o2 + 9 * 2 * 2 * 128].rearrange("p (i k a d) -> p i k a d", i=9, k=2, a=2)
        o3 = o2 + 9 * 2 * 2 * 128
        cosT = arena[:, o3:o3 + 2 * PTH].bitcast(F32)
        o4 = o3 + 2 * PTH
        sinT = arena[:, o4:o4 + 2 * PTH].bitcast(F32)
        o5 = o4 + 2 * PTH
        assert o5 <= 32768
        yout = arena[:, :].bitcast(F32).rearrange("p (i d) -> p i d", i=8)

        vn_t = sb("vn", [128, 8 * 1024], BF16)
        vn = vn_t[:, :].rearrange("p (i e) -> p i e", i=8)
        yTb = vn
        yTa_t = sb("yTa", [128, 8 * 1024], BF16)
        yTa = yTa_t[:, :].rearrange("p (i e) -> p i e", i=8)
        vraw = yTa_t[:, :].bitcast(F32).rearrange("p (i e) -> p i e", i=8)
        hb = [vn_t[:, 0:2048], vn_t[:, 2048:4096]]
        sqj = vn_t[:, 4096:6144]

        ring = [sb(f"ring{i}", [128, 16, UW], BF16) for i in range(NRING)]
        stg = [sb(f"stg{i}", [128, 8, UW], F32) for i in range(NSTG)]
        gbc = sb("gbc", [128, D], F32)
        scr = sb("scr", [128, 4608], F32)
        cscr = sb("cscr", [128, 9216], BF16)
        cinit = arena[:, o1:o3].bitcast(F32)
        ident_f = cinit[:, 0:128]
        pswap_f = cinit[:, 128:256]
        mss_f = cinit[:, 256:768]
        mfs_f = cinit[:, 768:1280]
        wst_f = cinit[:, 1280:2304]
        bs_bc = cinit[:, 2304:3328]
        xt = [cscr[:, 0:4096].bitcast(F32), cscr[:, 4096:8192].bitcast(F32), scr[:, 0:2048], scr[:, 2048:4096]]
        pos_t = sb("pos_i", [128, PTH], I32)
        pos_i = pos_t[:, :]
        ang = scr[:, 0:PTH]
        kk = scr[:, PTH:2 * PTH]
        rr = scr[:, 2 * PTH:3 * PTH]
        th = [scr[:, 0:512], scr[:, 512:1024]]
        mg = [scr[:, 1024:1536], scr[:, 1536:2048]]
        gz = [scr[:, 2048:3072], scr[:, 3072:4096]]
        ident_b = sb("ident_b", [128, 128], BF16)
        ones_b = sb("ones_b", [128, 128], BF16)
        sel_b = sb("sel_b", [128, 2, 128], BF16)
        pswap_b = sb("pswap_b", [128, 128], BF16)
        mss_b = sb("mss_b", [128, 512], BF16)
        mfs_b = sb("mfs_b", [128, 512], BF16)
        wst_b = sb("wst_b", [128, 1024], BF16)
        biasg = sb("biasg", [128, 1024], F32)
        cpa = sb("cpa_s", [128, 192], F32)
        bv_bc = cpa[:, 0:128]
        bq_fm = cpa[:, 128:136]
        bk_fm = cpa[:, 136:138]
        lng_fm = cpa[:, 138:146]
        lnb_fm = cpa[:, 146:154]
        sink_fm = cpa[:, 154:162]
        invf = cpa[:, 162:163]
        sgn = cpa[:, 163:164]
        gpre_fm = cpa[:, 164:180]
        bk_nat = cpa[:, 180:181]
        esink = cpa[:, 181:189]
        nhalf = cpa[:, 189:190]
        epsc = cpa[:, 190:191]
        stat = sb("stat", [128, 256], F32)
        ss0 = stat[:, 0:9]
        ms0 = stat[:, 9:18]
        rstd0 = stat[:, 18:27]
        bst = stat[:, 32:32 + 96].rearrange("p (i h s) -> p i h s", i=8, h=2)
        mv = stat[:, 128:144].rearrange("p (i s) -> p i s", i=8)
        vr = stat[:, 144:152]
        rstdv = stat[:, 152:160]
        ssq = stat[:, 160:224].rearrange("p (i v) -> p i v", i=8)
        ssd = stat[:, 224:232]
        msd = stat[:, 232:240]
        rstd2 = stat[:, 240:248]
        nmr = stat[:, 248:256]
        qf = [cscr[:, 0:1024].bitcast(F32), cscr[:, 1024:2048].bitcast(F32)]
        qtr = [cscr[:, 2048:3072], cscr[:, 3072:4096]]
        qb = [cscr[:, 4096:4608], cscr[:, 4608:5120]]
        ptA = [cscr[:, 5120:5632], cscr[:, 5632:6144]]
        ptB = [cscr[:, 6144:6656], cscr[:, 6656:7168]]
        dn = [cscr[:, 7168:7680].bitcast(F32), cscr[:, 7680:8192].bitcast(F32)]
        yn = [cscr[:, 8192:8704].bitcast(F32), cscr[:, 8704:9216].bitcast(F32)]
        sqj2 = sb("sqj2", [128, 256], BF16)
        t1 = qf
        t2 = mg
        az = th
        rd = dn

        pb = [es.enter_context(nc.psum_tensor(f"pb{i}", [128, 512], F32)) for i in range(8)]

        def mk(n, k=None):
            if k is None:
                return Buf(n)
            return [Buf(f"{n}{i}") for i in range(k)]

        B_pb = [Buf(f"pb{i}", excl=True) for i in range(8)]
        B_ring = mk("ring", NRING)
        B_stg = mk("stg", NSTG)
        B_cst = Buf("cst")
        B_gbc = Buf("gbc")

        B_xt = mk("xt", 2); B_hb = mk("hb", 2); B_sq = Buf("sq"); B_st = mk("stat0", 9); B_hT = mk("hT", 9)
        B_rope = Buf("rope"); B_VAB = Buf("VAB"); B_vraw = Buf("vraw"); B_vst = mk("vst", 8); B_vn = mk("vn", 8)
        B_yTb = B_vn
        B_KT = Buf("KT"); B_qf = mk("qf", 2); B_kf = B_qf; B_t1 = B_qf; B_qb = mk("qb", 2); B_kb = B_qb
        B_th = mk("th", 2); B_az = B_th; B_mg = mk("mg", 2); B_t2 = B_mg; B_yTa = mk("yTa", 8)
        B_qtr = mk("qtr", 2); B_gz = mk("gz", 2); B_ptA = mk("ptA", 2); B_ptB = mk("ptB", 2)
        B_dn = mk("dn", 2); B_rd = B_dn; B_yn = mk("yn", 2); B_yout = mk("yout", 8); B_ssq = mk("ssq", 2); B_ssq8 = mk("ssq8", 8)
        B_sqj2 = Buf("sqj2"); B_xr = mk("xr", 2); B_yo2 = mk("yo2", 8)

        B_c2 = Buf("c2")
        B_cid = Buf("cid")
        B_eps = Buf("eps")
        B_es = Buf("es")
        B_sel = Buf("sel")
        B_init = Buf("init")

        def emit_init():
            fw.op("pool", lambda h: h.memset(nhalf, -0.5), reads=[], writes=[B_eps])
            fw.op("pool", lambda h: h.memset(epsc, EPS), reads=[], writes=[B_eps])
            fw.dma("sp", cpa[:, 0:181], cpa_d, "cst", writes=[B_cst])
            fw.dma("sp", cinit[:, 0:3328], cpb_d, "cst2", writes=[B_init])
            fw.op("pool", lambda h: h.tensor_copy(out=ident_b[:], in_=ident_f), reads=[B_init], writes=[B_cid])

        def emit_init_late():
            fw.op("pool", lambda h: h.tensor_copy(out=pswap_b[:], in_=pswap_f), reads=[B_init], writes=[B_c2])
            fw.op("pool", lambda h: h.tensor_scalar(out=mss_b[:], in0=mss_f, scalar1=-1.0, scalar2=MASKNEG, op0=ALU.add, op1=ALU.mult), reads=[B_init], writes=[B_c2])
            fw.op("pool", lambda h: h.tensor_scalar(out=mfs_b[:], in0=mfs_f, scalar1=-1.0, scalar2=MASKNEG, op0=ALU.add, op1=ALU.mult), reads=[B_init], writes=[B_c2])
            fw.op("pool", lambda h: h.memset(ones_b[:], 1.0), writes=[B_c2])
            fw.op("pool", lambda h: h.memset(sel_b[:], 0.0), writes=[B_sel])
            for v_ in range(2):
                for hm in range(2):
                    fw.op("pool", lambda h, v_=v_, hm=hm: h.tensor_copy(out=sel_b[64 * v_:64 * v_ + 64, v_, 64 * hm:64 * hm + 64],
                                                                     in_=ident_b[64 * v_:64 * v_ + 64, 64 * v_:64 * v_ + 64]),
                          reads=[B_cid], writes=[B_sel])
            mcur_f = mss_f[:, 128:256]
            wst3 = wst_f.rearrange("p (g t) -> p g t", g=8)
            fw.op("dve", lambda h: h.tensor_tensor(out=wst3, in0=wst3, in1=mcur_f.unsqueeze(1).to_broadcast([128, 8, 128]), op=ALU.mult),
                  reads=[], writes=[B_init])
            fw.op("dve", lambda h: h.tensor_copy(out=wst_b[:], in_=wst_f), reads=[B_init], writes=[B_c2])
            for hf in range(2):
                def f(h, hf=hf):
                    return h.matmul(pb[hf][:], lhsT=ones_b[:], rhs=wst_b[:, hf * 512:(hf + 1) * 512], start=True, stop=True)
                fw.op("pe", f, reads=[B_c2], writes=[B_pb[hf]])
                for gg in range(4):
                    g = hf * 4 + gg
                    fw.op("dve", lambda h, g=g, gg=gg, hf=hf: h.scalar_tensor_tensor(
                        out=biasg[:, g * 128:(g + 1) * 128], in0=pb[hf][:, gg * 128:(gg + 1) * 128], scalar=lnb_fm[:, g:g + 1],
                        in1=bs_bc[:, g * 128:(g + 1) * 128], op0=ALU.mult, op1=ALU.add), reads=[B_pb[hf], B_cst, B_init], writes=[B_c2])
            fw.op("act", lambda h: h.activation(out=esink, in_=sink_fm, func=AF.Exp), reads=[B_cst], writes=[B_es])

        B_ycp = [Buf("ycp0"), Buf("ycp1")]

        units = []
        for u in [4, 5, 6, 7, 16, 0, 8, 1, 9, 2, 10, 3, 11, 12, 17, 13, 18, 14, 19, 15, 20]:
            units.append(("in", u * UW))
        for v in range(8):
            units.append(("out", v * UW))
        NU = len(units)

        NHALF = NPASS * NU * 2
        wstate = {"dma": 0, "cast": 0, "allowed": -1}

        def _half_info(hidx):
            gi, hh = hidx // 2, hidx % 2
            kind, c0 = units[gi % NU]
            return gi, hh, kind, c0

        def _emit_dma():
            hidx = wstate["dma"]
            gi, hh, kind, c0 = _half_info(hidx)
            s_ = hidx % NSTG
            wsrc = win_d if kind == "in" else wout_d
            src = wsrc[hh * 1024:(hh + 1) * 1024, c0:c0 + UW].rearrange("(c p) e -> p c e", p=128)
            fw.dma("sp", stg[s_][:], src, f"stg{s_}", writes=[B_stg[s_]])
            wstate["dma"] += 1

        def _emit_cast():
            hidx = wstate["cast"]
            gi, hh, kind, c0 = _half_info(hidx)
            s_ = hidx % NSTG
            slot = gi % NRING
            dst = ring[slot][:, hh * 8:(hh + 1) * 8, :]
            if kind == "in":
                for c in range(8):
                    last = (c == 7)
                    fw.op("pool", lambda h, c=c, dst=dst, s_=s_, hh=hh: h.tensor_scalar(out=dst[:, c, :], in0=stg[s_][:, c, :],
                                                                                      scalar1=gpre_fm[:, 8 * hh + c:8 * hh + c + 1], scalar2=1.0,
                                                                                      op0=ALU.mult, op1=ALU.mult),
                          reads=[B_stg[s_], B_cst], writes=[B_ring[slot]])
            else:
                fw.op("pool", lambda h, dst=dst, s_=s_: h.tensor_scalar(out=dst, in0=stg[s_][:], scalar1=0.5, scalar2=1.0,
                                                                        op0=ALU.mult, op1=ALU.mult), reads=[B_stg[s_]], writes=[B_ring[slot]])
            wstate["cast"] += 1

        def _can_dma():
            return wstate["dma"] < NHALF and wstate["dma"] - wstate["cast"] < NSTG

        def _can_cast():
            return wstate["cast"] < wstate["dma"] and (wstate["cast"] // 2) <= wstate["allowed"]

        def pump(n=1):
            for _ in range(n):
                if _can_cast():
                    _emit_cast()
                if _can_dma():
                    _emit_dma()

        def allow(gi):
            wstate["allowed"] = max(wstate["allowed"], gi)

        def need(gi):
            allow(gi)
            tgt = min(2 * (gi + 1), NHALF)
            while wstate["cast"] < tgt:
                if _can_cast():
                    _emit_cast()
                elif _can_dma():
                    _emit_dma()
                else:
                    raise RuntimeError("weight stream stuck")

        TWO_PI = 2.0 * np.pi
        C1 = 6.28125
        C2 = float(np.float32(TWO_PI - C1))
        C3 = float(TWO_PI - C1 - np.float64(np.float32(TWO_PI - C1)))
        MAGIC = 12582912.0
        PI_S = 3.1415925

        xpref = {"done": False}

        def emit_all():
          if stop == 'init':
            raise _Stop()
          for ps in range(NPASS):
            r0 = ps * PT
            gbase = ps * NU
            scrB = [B_th[0], B_th[1], B_mg[0], B_mg[1], B_gz[0], B_gz[1]]
            xtB = [[B_qf[0], B_qf[1], B_qtr[0], B_qtr[1]],
                   [B_qb[0], B_qb[1], B_ptA[0], B_ptA[1], B_ptB[0], B_ptB[1], B_dn[0], B_dn[1]],
                   [B_th[0], B_th[1], B_mg[0], B_mg[1]],
                   [B_gz[0], B_gz[1]]]

            def a_dma(i, ps_=ps):
                b = i % 4
                extra = []
                rr0 = ps_ * PT
                fw.dma("act", xt[b], x_d[rr0 + 128 * i:rr0 + 128 * (i + 1), :], f"xt{b}", writes=xtB[b] + extra)

            if ps == 0:
                a_dma(0)
                a_dma(1)
                a_dma(2)
                emit_init()
                a_dma(3)
                fw.dma("sp", pos_i, pos_d[0:1, r0:r0 + PTH].partition_broadcast(128), "pos", writes=[B_rope])
                need(gbase + 0)
                xpref["done"] = True
            else:
                fw.dma("sp", pos_i, pos_d[0:1, r0:r0 + PTH].partition_broadcast(128), "pos", writes=[B_rope])
                need(gbase + 1)
            hbB = [B_vn[0:2], B_vn[2:4]]
            sqB = B_vn[4:6]
            rope_ops = []

            def R(en, fn):
                rope_ops.append((en, fn))
            R("dve", lambda h: h.tensor_copy(out=ang, in_=pos_i))
            R("dve", lambda h: h.tensor_scalar(out=ang, in0=ang, scalar1=invf, scalar2=None, op0=ALU.mult))
            R("dve", lambda h: h.tensor_scalar(out=kk, in0=ang, scalar1=float(1.0 / TWO_PI), scalar2=MAGIC, op0=ALU.mult, op1=ALU.add))
            R("dve", lambda h: h.tensor_scalar(out=kk, in0=kk, scalar1=-MAGIC, scalar2=None, op0=ALU.add))
            R("dve", lambda h: h.scalar_tensor_tensor(out=rr, in0=kk, scalar=-C1, in1=ang, op0=ALU.mult, op1=ALU.add))
            R("dve", lambda h: h.scalar_tensor_tensor(out=rr, in0=kk, scalar=-C2, in1=rr, op0=ALU.mult, op1=ALU.add))
            R("dve", lambda h: h.scalar_tensor_tensor(out=rr, in0=kk, scalar=-C3, in1=rr, op0=ALU.mult, op1=ALU.add))
            R("dve", lambda h: h.tensor_scalar(out=kk, in0=rr, scalar1=float(np.pi), scalar2=-TWO_PI, op0=ALU.is_gt, op1=ALU.mult))
            R("dve", lambda h: h.tensor_tensor(out=rr, in0=rr, in1=kk, op=ALU.add))
            R("dve", lambda h: h.tensor_scalar(out=kk, in0=rr, scalar1=float(-np.pi), scalar2=TWO_PI, op0=ALU.is_lt, op1=ALU.mult))
            R("dve", lambda h: h.tensor_tensor(out=rr, in0=rr, in1=kk, op=ALU.add))
            R("dve", lambda h: h.tensor_scalar(out=ang, in0=rr, scalar1=PI_S, scalar2=-PI_S, op0=ALU.min, op1=ALU.max))
            R("dve", lambda h: h.tensor_scalar(out=rr, in0=rr, scalar1=float(np.pi / 2), scalar2=None, op0=ALU.add))
            R("dve", lambda h: h.tensor_scalar(out=kk, in0=rr, scalar1=float(np.pi), scalar2=-TWO_PI, op0=ALU.is_gt, op1=ALU.mult))
            R("dve", lambda h: h.tensor_tensor(out=rr, in0=rr, in1=kk, op=ALU.add))
            R("dve", lambda h: h.tensor_scalar(out=rr, in0=rr, scalar1=PI_S, scalar2=-PI_S, op0=ALU.min, op1=ALU.max))

            def emit_rope(n):
                for _ in range(n):
                    if rope_ops:
                        en, fn = rope_ops.pop(0)
                        fw.op(en, fn, reads=[B_cst], writes=[B_rope] + scrB)

            def a_sq(i):
                b = i % 4
                fw.op("act", lambda h: h.activation(out=sqj, in_=xt[b], func=AF.Square, accum_out=ss0[:, i:i + 1]),
                      reads=xtB[b], writes=sqB + [B_st[i]])
                fw.op("act", lambda h: h.activation(out=ms0[:, i:i + 1], in_=ss0[:, i:i + 1], func=AF.Sqrt, bias=epsc, scale=1.0 / D),
                      reads=[B_eps], writes=[B_st[i]])

            def a_rc(i):
                fw.op("dve", lambda h: h.reciprocal(out=rstd0[:, i:i + 1], in_=ms0[:, i:i + 1]), reads=[], writes=[B_st[i]])

            def a_stt(i):
                b = i % 4
                b2 = i % 2
                fw.op("dve", lambda h: h.tensor_scalar(out=hb[b2], in0=xt[b], scalar1=rstd0[:, i:i + 1], scalar2=None, op0=ALU.mult),
                      reads=xtB[b] + [B_st[i]], writes=hbB[b2])

            def pview(i, g8):
                bank = 2 * (i % 2) + g8
                return bank, pb[bank][:, :].bitcast(BF16).rearrange("p (c t) -> p c t", c=8)

            def b_T(i):
                b2 = i % 2
                for g8 in range(2):
                    bank, pv = pview(i, g8)

                    def f(h, g8=g8, pv=pv):
                        ins = None
                        for c in range(8):
                            cc = g8 * 8 + c
                            ins = h.transpose(pv[:, c, :], hb[b2][:, cc * 128:(cc + 1) * 128], ident_b[:])
                        return ins
                    fw.op("pe", f, reads=hbB[b2] + [B_cid], writes=[B_pb[bank]])

            def b_cp(i):
                for g8 in range(2):
                    bank, pv = pview(i, g8)
                    dst = hT[:, g8 * 8:(g8 + 1) * 8, i * 128:(i + 1) * 128]
                    if g8 == 0:
                        fw.op("act", lambda h, dst=dst, pv=pv: h.copy(out=dst, in_=pv), reads=[B_pb[bank]], writes=[B_hT[i]] + B_yout[0:5])
                    else:
                        fw.op("dve", lambda h, dst=dst, pv=pv: h.tensor_copy(out=dst, in_=pv), reads=[B_pb[bank]], writes=[B_hT[i]] + B_yout[0:5])

            def c_pe(hh, i):
                ua, ub = gbase + 2 * hh, gbase + 2 * hh + 1
                bank = 4 + (i % 2)

                def f(h):
                    ins = None
                    for uu, gu in enumerate((ua, ub)):
                        for c in range(16):
                            ins = h.matmul(pb[bank][:, uu * UW:(uu + 1) * UW], lhsT=hT[:, c, i * 128:(i + 1) * 128],
                                           rhs=ring[gu % NRING][:, c, :], start=(c == 0), stop=(c == 15))
                    return ins
                fw.op("pe", f, reads=[B_hT[i], B_ring[ua % NRING], B_ring[ub % NRING]], writes=[B_pb[bank]])

            def c_post(hh, i):
                bank = 4 + (i % 2)
                vs = B_vst[i - 1]
                fw.op("dve", lambda h: h.bn_stats(out=bst[:, i - 1, hh, :], in_=pb[bank][:]), reads=[B_pb[bank]], writes=[vs])
                if hh == 0:
                    fw.op("act", lambda h: h.copy(out=vraw[:, i - 1, :], in_=pb[bank][:]), reads=[B_pb[bank]], writes=[B_yTa[i - 1]])
                else:
                    fw.op("dve", lambda h: h.bn_aggr(out=mv[:, i - 1, :], in_=bst[:, i - 1, :, :].rearrange("p h s -> p (h s)")),
                          reads=[], writes=[vs])
                    fw.op("act", lambda h: h.activation(out=vr[:, i - 1:i], in_=mv[:, i - 1, 1:2], func=AF.Sqrt, bias=epsc, scale=1.0),
                          reads=[B_eps], writes=[vs])
                    fw.op("dve", lambda h: h.reciprocal(out=rstdv[:, i - 1:i], in_=vr[:, i - 1:i]), reads=[], writes=[vs])
                    fw.op("dve", lambda h: h.scalar_tensor_tensor(out=nmr[:, i - 1:i], in0=mv[:, i - 1, 0:1], scalar=-1.0, in1=rstdv[:, i - 1:i],
                                                                  op0=ALU.mult, op1=ALU.mult), reads=[], writes=[vs])
                    fw.op("act", lambda h: h.activation(out=vn[:, i - 1, 0:512], in_=vraw[:, i - 1, :], func=AF.Identity,
                                                        bias=nmr[:, i - 1:i], scale=rstdv[:, i - 1:i]),
                          reads=[B_yTa[i - 1], vs], writes=[B_vn[i - 1]])
                    fw.op("act", lambda h: h.activation(out=vn[:, i - 1, 512:1024], in_=pb[bank][:], func=AF.Identity,
                                                        bias=nmr[:, i - 1:i], scale=rstdv[:, i - 1:i]),
                          reads=[B_pb[bank], vs], writes=[B_vn[i - 1]])

            if not xpref["done"]:
                for j in range(3):
                    a_dma(j)
            xpref["done"] = False
            for t in range(-1, 12):
                if 4 <= t + 3 <= 8:
                    a_dma(t + 3)
                if t == 2:
                    need(gbase + 1)
                if 0 <= t + 1 <= 8:
                    a_sq(t + 1)
                if t == 3:
                    allow(gbase + 3)
                pump()
                if 0 <= t <= 8:
                    a_stt(t)
                if 0 <= t - 1 <= 8:
                    b_T(t - 1)
                if 1 <= t - 3 <= 8:
                    c_post(0, t - 3)
                if 0 <= t - 1 <= 8:
                    b_cp(t - 1)
                if 0 <= t + 1 <= 8:
                    a_rc(t + 1)
                if 1 <= t - 2 <= 8:
                    c_pe(0, t - 2)

            def phaseA_tile(hh, i):
                c_pe(hh, i)
                c_post(hh, i)

            need(gbase + 3)
            allow(gbase + 5)
            if ps == 0:
                emit_init_late()
            fw.op("pool", lambda h: h.memset(arena[:, o2:o3], 1.0), writes=[B_VAB, B_init] + B_yout[5:7])
            for i in range(1, 9):
                phaseA_tile(1, i)
                emit_rope(2)
                pump()
            emit_rope(99)
            fw.op("act", lambda h: h.activation(out=sinT, in_=ang, func=AF.Sin, scale=sgn), reads=[B_cst] + scrB, writes=[B_rope] + B_yout[6:8])
            fw.op("act", lambda h: h.activation(out=cosT, in_=rr, func=AF.Sin), reads=scrB, writes=[B_rope] + B_yout[6:8])
            if debug and ps == 0:
                fw.dma("sp", dbg["dbg_hT"], arena[:, 0:16 * PTH], "dbg", reads=B_hT)
                fw.dma("sp", dbg["dbg_cs"], arena[:, o3:o5].bitcast(F32), "dbg", reads=[B_rope])
            if stop == 'A' and ps == 0:
                raise _Stop()
            ukv = gbase + 4
            need(ukv)
            allow(ukv + 3)
            def v_tile(i):
                bank = i % 2

                def f(h, i=i, bank=bank, ukv=ukv):
                    ins = None
                    for c in range(16):
                        ins = h.matmul(pb[bank][:, 0:128], lhsT=hT[:, c, i * 128:(i + 1) * 128], rhs=ring[ukv % NRING][:, c, 128:256],
                                       start=(c == 0), stop=(c == 15))
                    return ins
                fw.op("pe", f, reads=B_hT + [B_ring[ukv % NRING]], writes=[B_pb[bank]])
                pv3 = pb[bank][:, 0:128].rearrange("p (k d) -> p k d", k=2)
                bv3 = bv_bc[:, :].rearrange("p (k d) -> p k d", k=2)
                fw.op("dve", lambda h, i=i, pv3=pv3, bv3=bv3: h.tensor_tensor(out=VAB[:, i, :, 0, 0:64], in0=pv3, in1=bv3, op=ALU.add),
                      reads=[B_pb[bank], B_cst], writes=[B_VAB])
                fw.op("dve", lambda h, i=i, pv3=pv3, bv3=bv3: h.tensor_tensor(out=VAB[:, i, :, 1, 64:128], in0=pv3, in1=bv3, op=ALU.add),
                      reads=[B_pb[bank], B_cst], writes=[B_VAB])
                pump()
            slabs3 = [(0, 512), (512, 512), (1024, 128)]
            ktmp = qtr

            def k1(s_):
                t0, n = slabs3[s_]
                b = s_ % 2
                bank = (2, 5)[b]

                def f(h, ukv=ukv):
                    ins = None
                    for c in range(16):
                        ins = h.matmul(pb[bank][:, 0:n], lhsT=ring[ukv % NRING][:, c, 0:128], rhs=hT[:, c, t0:t0 + n],
                                       start=(c == 0), stop=(c == 15))
                    return ins
                fw.op("pe", f, reads=B_hT + [B_ring[ukv % NRING]], writes=[B_pb[bank]])
                fw.op("act", lambda h: h.activation(out=qf[b][:, 0:n], in_=pb[bank][:, 0:n], func=AF.Identity, bias=bk_nat),
                      reads=[B_pb[bank], B_cst], writes=[B_kf[b]])
                fw.op("act", lambda h: h.activation(out=qb[b][:, 0:n], in_=pb[bank][:, 0:n], func=AF.Identity, bias=bk_nat),
                      reads=[B_pb[bank], B_cst], writes=[B_kb[b]])

            def k2(s_):
                t0, n = slabs3[s_]
                b = s_ % 2
                fw.op("pe", lambda h: h.matmul(pb[6][:, 0:n], lhsT=pswap_b[:], rhs=qb[b][:, 0:n], start=True, stop=True),
                      reads=[B_kb[b], B_c2], writes=[B_pb[6]])
                fw.op("dve", lambda h: h.tensor_tensor(out=t1[b][:, 0:n], in0=qf[b][:, 0:n], in1=cosT[:, t0:t0 + n], op=ALU.mult),
                      reads=[B_rope], writes=[B_t1[b]])
                fw.op("dve", lambda h: h.tensor_tensor(out=t2[b][:, 0:n], in0=pb[6][:, 0:n], in1=sinT[:, t0:t0 + n], op=ALU.mult),
                      reads=[B_pb[6], B_rope], writes=[B_t2[b]])
                fw.op("dve", lambda h: h.tensor_tensor(out=ktmp[b][:, 0:n], in0=t1[b][:, 0:n], in1=t2[b][:, 0:n], op=ALU.add),
                      reads=[B_t1[b], B_t2[b]], writes=[B_qtr[b]])

            def k3(s_):
                t0, n = slabs3[s_]
                b = s_ % 2
                for kv in range(2):
                    bk_ = 7 if kv == 0 else 3
                    fw.op("pe", lambda h, kv=kv, bk_=bk_: h.matmul(pb[bk_][:, 0:n], lhsT=sel_b[:, kv, :], rhs=ktmp[b][:, 0:n], start=True, stop=True),
                          reads=[B_qtr[b], B_sel], writes=[B_pb[bk_]])
                    if kv == 0:
                        fw.op("act", lambda h, kv=kv, bk_=bk_: h.copy(out=KT[:, kv, t0:t0 + n], in_=pb[bk_][:, 0:n]),
                              reads=[B_pb[bk_]], writes=[B_KT, B_init] + B_yout[4:6])
                    else:
                        fw.op("dve", lambda h, kv=kv, bk_=bk_: h.tensor_copy(out=KT[:, kv, t0:t0 + n], in_=pb[bk_][:, 0:n]),
                              reads=[B_pb[bk_]], writes=[B_KT, B_init] + B_yout[4:6])

            kjobs = slabs3
            vq = list(range(9))
            for s_ in range(len(kjobs) + 2):
                if s_ < len(kjobs):
                    k1(s_)
                if vq:
                    v_tile(vq.pop(0))
                if 1 <= s_ <= len(kjobs):
                    k2(s_ - 1)
                if vq:
                    v_tile(vq.pop(0))
                if s_ >= 2:
                    k3(s_ - 2)
                pump()
            while vq:
                v_tile(vq.pop(0))
            if debug and ps == 0:
                fw.dma("sp", dbg["dbg_vn"], vn_t[:, :], "dbg", reads=B_vn)
                fw.dma("sp", dbg["dbg_KT"], arena[:, o1:o2], "dbg", reads=[B_KT])

            if stop == 'A3' and ps == 0:
                raise _Stop()
            for q2 in range(2):
                rows = ps * PT + q2 * 512
                fw.dma("pool", y_d[rows:rows + 512, :], x_d[HALO + rows:HALO + rows + 512, :], f"ycp{q2}", writes=[B_ycp[q2]])
            if ps == 0:
                fw.dma("pool", gbc[:], gpost_d.partition_broadcast(128), "gbc", writes=[B_gbc])
            itb = 0
            for j in range(4):
                uu_, uz_ = gbase + 5 + 2 * j, gbase + 6 + 2 * j
                need(uz_)
                allow(uz_ + 2)
                su, sz = uu_ % NRING, uz_ % NRING
                for gg in range(2):
                    g = 2 * j + gg
                    for sl in range(2):
                        t0 = HALO + 512 * sl
                        b = itb % 2
                        itb += 1
                        bU, bZ, bM = b, 2 + b, 4 + b

                        def fu(h, su=su, gg=gg, t0=t0, bU=bU):
                            ins = None
                            for c in range(16):
                                ins = h.matmul(pb[bU][:], lhsT=ring[su][:, c, gg * 128:(gg + 1) * 128], rhs=hT[:, c, t0:t0 + 512],
                                               start=(c == 0), stop=(c == 15))
                            return ins
                        fw.op("pe", fu, reads=B_hT + [B_ring[su]], writes=[B_pb[bU]])

                        def fz(h, sz=sz, gg=gg, t0=t0, bZ=bZ):
                            ins = None
                            for c in range(16):
                                ins = h.matmul(pb[bZ][:], lhsT=ring[sz][:, c, gg * 128:(gg + 1) * 128], rhs=hT[:, c, t0:t0 + 512],
                                               start=(c == 0), stop=(c == 15))
                            return ins
                        fw.op("pe", fz, reads=B_hT + [B_ring[sz]], writes=[B_pb[bZ]])

                        def fm(h, g=g, sl=sl, bM=bM):
                            ins = None
                            for ch in range(4):
                                ins = h.matmul(pb[bM][:, ch * 128:(ch + 1) * 128], lhsT=vn[:, 4 * sl + ch, g * 128:(g + 1) * 128],
                                               rhs=wst_b[:, g * 128:(g + 1) * 128], start=True, stop=True)
                            return ins
                        fw.op("pe", fm, reads=[B_vn[4 * sl + c_] for c_ in range(4)] + [B_c2], writes=[B_pb[bM]])
                        fw.op("act", lambda h, b=b, bZ=bZ: h.activation(out=th[b], in_=pb[bZ][:], func=AF.Tanh, scale=0.5),
                              reads=[B_pb[bZ]], writes=[B_th[b]])
                        fw.op("dve", lambda h, b=b, bZ=bZ: h.scalar_tensor_tensor(out=az[b], in0=th[b], scalar=1.0, in1=pb[bZ][:],
                                                                                 op0=ALU.add, op1=ALU.mult), reads=[B_th[b], B_pb[bZ]], writes=[B_az[b]])
                        fw.op("dve", lambda h, b=b, bM=bM, g=g: h.scalar_tensor_tensor(
                            out=mg[b].rearrange("p (c t) -> p c t", c=4), in0=pb[bM][:, :].rearrange("p (c t) -> p c t", c=4),
                            scalar=lng_fm[:, g:g + 1], in1=biasg[:, g * 128:(g + 1) * 128].unsqueeze(1).to_broadcast([128, 4, 128]),
                            op0=ALU.mult, op1=ALU.add), reads=[B_pb[bM], B_c2, B_cst], writes=[B_mg[b]])
                        fw.op("dve", lambda h, b=b, bU=bU: h.tensor_tensor(out=mg[b], in0=mg[b], in1=pb[bU][:], op=ALU.mult),
                              reads=[B_mg[b], B_pb[bU]], writes=[B_mg[b]])
                        fw.op("dve", lambda h, b=b, g=g, sl=sl: h.tensor_tensor(out=yTa[:, g, sl * 512:(sl + 1) * 512], in0=mg[b], in1=az[b], op=ALU.mult),
                              reads=[B_mg[b], B_az[b]], writes=[B_yTa[g]])
                        pump()
            if debug and ps == 0:
                fw.dma("sp", dbg["dbg_yTa"], yTa_t[:, :], "dbg", reads=B_yTa)

            if stop == 'B' and ps == 0:
                raise _Stop()
            def proj_steps(ch, gbase=gbase):
                j, cc = ch // 2, ch % 2
                uq_, uzb_ = gbase + 13 + 2 * j, gbase + 14 + 2 * j
                sq, szb = uq_ % NRING, uzb_ % NRING
                cb = ch % 2

                def q_part(sl):
                    t0 = HALO + 512 * sl
                    bq_ = 0 if sl == 0 else 7
                    if cc == 0 and sl == 0:
                        need(uzb_)
                        allow(uzb_ + 2)

                    def fq(h):
                        ins = None
                        for c in range(16):
                            ins = h.matmul(pb[bq_][:], lhsT=ring[sq][:, c, cc * 128:(cc + 1) * 128], rhs=hT[:, c, t0:t0 + 512],
                                           start=(c == 0), stop=(c == 15))
                        return ins
                    fw.op("pe", fq, reads=B_hT + [B_ring[sq]], writes=[B_pb[bq_]])
                    fw.op("act", lambda h: h.activation(out=qf[sl][:], in_=pb[bq_][:], func=AF.Identity, bias=bq_fm[:, ch:ch + 1]),
                          reads=[B_pb[bq_], B_cst], writes=[B_qf[sl]])
                    fw.op("act", lambda h: h.activation(out=qb[sl][:], in_=pb[bq_][:], func=AF.Identity, bias=bq_fm[:, ch:ch + 1]),
                          reads=[B_pb[bq_], B_cst], writes=[B_qb[sl]])

                def rot_part(sl):
                    t0 = HALO + 512 * sl
                    fw.op("pe", lambda h: h.matmul(pb[1][:], lhsT=pswap_b[:], rhs=qb[sl][:], start=True, stop=True),
                          reads=[B_qb[sl], B_c2], writes=[B_pb[1]])
                    fw.op("pool", lambda h: h.tensor_tensor(out=qf[sl][:], in0=qf[sl][:], in1=cosT[:, t0:t0 + 512], op=ALU.mult),
                          reads=[B_rope], writes=[B_qf[sl]])
                    fw.op("dve", lambda h: h.tensor_tensor(out=t2[sl], in0=pb[1][:], in1=sinT[:, t0:t0 + 512], op=ALU.mult),
                          reads=[B_pb[1], B_rope], writes=[B_t2[sl]])
                    fw.op("dve", lambda h: h.tensor_tensor(out=qtr[cb][:, sl * 512:(sl + 1) * 512], in0=qf[sl][:], in1=t2[sl], op=ALU.add),
                          reads=[B_qf[sl], B_t2[sl]], writes=[B_qtr[cb]])

                def z_part(sl):
                    t0 = HALO + 512 * sl

                    def fzb(h):
                        ins = None
                        for c in range(16):
                            ins = h.matmul(pb[2][:], lhsT=ring[szb][:, c, cc * 128:(cc + 1) * 128], rhs=hT[:, c, t0:t0 + 512],
                                           start=(c == 0), stop=(c == 15))
                        return ins
                    fw.op("pe", fzb, reads=B_hT + [B_ring[szb]], writes=[B_pb[2]])
                    fw.op("act", lambda h: h.activation(out=th[sl], in_=pb[2][:], func=AF.Tanh, scale=0.5), reads=[B_pb[2]], writes=[B_th[sl]])
                    fw.op("dve", lambda h: h.scalar_tensor_tensor(out=gz[cb][:, sl * 512:(sl + 1) * 512], in0=th[sl], scalar=1.0, in1=pb[2][:],
                                                                  op0=ALU.add, op1=ALU.mult), reads=[B_th[sl], B_pb[2]], writes=[B_gz[cb]])

                def s0():
                    q_part(0)

                def s1():
                    q_part(1)
                    rot_part(0)

                def s2():
                    z_part(0)
                    rot_part(1)

                def s3():
                    z_part(1)
                return [s0, s1, s2, s3]

            def attn_steps(ch, ps=ps):
                kv = ch // 4
                cb = ch % 2
                steps = []
                for bp in range(4):
                    pbuf = bp % 2
                    bSA, bSB, bO = 3, 4, 5 + pbuf
                    mask = mfs_b if (ps == 0 and bp == 0) else mss_b

                    def step_s(kv=kv, cb=cb, bp=bp, pbuf=pbuf, bSA=bSA, bSB=bSB, mask=mask):
                        def fs(h):
                            ins = None
                            for (r0_, bank) in ((0, bSA), (64, bSB)):
                                h.matmul(pb[bank][:], lhsT=ident_b[:], rhs=mask[:], start=True, stop=False)
                                for blk in range(2):
                                    bl = 2 * bp + blk
                                    for kc in range(2):
                                        ktile = bl + kc
                                        col = (blk * 2 + kc) * 128
                                        ins = h.matmul(pb[bank][:, col:col + 128], lhsT=KT[r0_:r0_ + 64, kv, ktile * 128:(ktile + 1) * 128],
                                                       rhs=qtr[cb][r0_:r0_ + 64, bl * 128:(bl + 1) * 128], start=False,
                                                       stop=(blk == 1 and kc == 1))
                            return ins
                        fw.op("pe", fs, reads=[B_KT, B_qtr[cb], B_c2, B_cid], writes=[B_pb[bSA], B_pb[bSB]])
                        fw.op("act", lambda h: h.activation(out=ptA[pbuf][:], in_=pb[bSA][:], func=AF.Exp, scale=0.125),
                              reads=[B_pb[bSA]], writes=[B_ptA[pbuf]])
                        fw.op("act", lambda h: h.activation(out=ptB[pbuf][:], in_=pb[bSB][:], func=AF.Exp, scale=0.125),
                              reads=[B_pb[bSB]], writes=[B_ptB[pbuf]])

                    def step_pv(kv=kv, cb=cb, bp=bp, pbuf=pbuf, bO=bO, ch=ch):
                        def fpv(h):
                            ins = None
                            for (a_, pt) in ((0, ptA), (1, ptB)):
                                for blk in range(2):
                                    bl = 2 * bp + blk
                                    for kc in range(2):
                                        vt = bl + kc
                                        col = (blk * 2 + kc) * 128
                                        ins = h.matmul(pb[bO][:, a_ * 256 + blk * 128:a_ * 256 + (blk + 1) * 128], lhsT=VAB[:, vt, kv, a_, :],
                                                       rhs=pt[pbuf][:, col:col + 128], start=(kc == 0), stop=(kc == 1))
                            return ins
                        fw.op("pe", fpv, reads=[B_VAB, B_ptA[pbuf], B_ptB[pbuf]], writes=[B_pb[bO]])
                        fw.op("act", lambda h: h.activation(out=dn[pbuf][64:128, :], in_=pb[bO][64:128, 0:256], func=AF.Identity, bias=esink[64:128, ch:ch + 1]),
                              reads=[B_pb[bO], B_es], writes=[B_dn[pbuf]])
                        fw.op("act", lambda h: h.activation(out=dn[pbuf][0:64, :], in_=pb[bO][0:64, 256:512], func=AF.Identity, bias=esink[0:64, ch:ch + 1]),
                              reads=[B_pb[bO], B_es], writes=[B_dn[pbuf]])
                        fw.op("dve", lambda h: h.reciprocal(out=rd[pbuf][:], in_=dn[pbuf][:]), reads=[], writes=[B_dn[pbuf]])
                        fw.op("dve", lambda h: h.tensor_tensor(out=yn[pbuf][0:64, :], in0=pb[bO][0:64, 0:256], in1=rd[pbuf][64:128, :], op=ALU.mult),
                              reads=[B_pb[bO], B_rd[pbuf]], writes=[B_yn[pbuf]])
                        fw.op("dve", lambda h: h.tensor_tensor(out=yn[pbuf][64:128, :], in0=pb[bO][64:128, 256:512], in1=rd[pbuf][0:64, :], op=ALU.mult),
                              reads=[B_pb[bO], B_rd[pbuf]], writes=[B_yn[pbuf]])
                        fw.op("dve", lambda h: h.tensor_tensor(out=yTb[:, ch, bp * 256:(bp + 1) * 256], in0=yn[pbuf][:],
                                                               in1=gz[cb][:, bp * 256:(bp + 1) * 256], op=ALU.mult),
                              reads=[B_yn[pbuf], B_gz[cb]], writes=[B_yTb[ch]])
                    steps += [step_s, step_pv]
                return steps

            dfill = {}

            def fill_steps(gbase=gbase):
                uo = gbase + 21
                so = uo % NRING
                steps = []
                for k_, bank in enumerate((0, 1, 2, 7, 3, 4, 5)):
                    def st(k_=k_, bank=bank):
                        if k_ == 0:
                            need(uo)

                        def fo(h):
                            ins = None
                            for c in range(15):
                                src = yTa if c < 8 else yTb
                                ins = h.matmul(pb[bank][:, 0:UW], lhsT=src[:, c % 8, k_ * 128:(k_ + 1) * 128], rhs=ring[so][:, c, :],
                                               start=(c == 0), stop=False)
                            return ins
                        fw.op("pe", fo, reads=[B_ring[so]] + B_yTa + B_yTb[0:7], writes=[B_pb[bank]])
                        dfill[k_] = bank
                    steps.append(st)
                return steps

            for st_ in proj_steps(0):
                st_()
                pump()
            for ch in range(8):
                A_ = attn_steps(ch)
                P_ = proj_steps(ch + 1) if ch + 1 < 8 else fill_steps()
                if ch + 1 < 8:
                    order = [A_[0]] + P_[0:1] + [A_[1], A_[2]] + P_[1:2] + [A_[3], A_[4]] + P_[2:3] + [A_[5], A_[6]] + P_[3:4] + [A_[7]]
                else:
                    order = [A_[0], A_[1], A_[2]] + P_[0:1] + [A_[3], A_[4]] + P_[1:2] + [A_[5], A_[6]] + P_[2:3] + [A_[7]] + P_[3:7]
                for st_ in order:
                    st_()
                    pump()
            if debug and ps == 0:
                fw.dma("sp", dbg["dbg_yTb"], vn_t[:, :], "dbg", reads=B_yTb)

            if stop == 'C' and ps == 0:
                raise _Stop()
            nonlocal_itd = [0]
            arenaB = {0: B_hT, 1: B_hT, 2: B_hT, 3: B_hT, 4: B_hT + [B_KT], 5: [B_KT, B_VAB], 6: [B_VAB, B_rope], 7: [B_rope]}
            if ps + 1 < NPASS:
                for j in range(4):
                    a_dma(j, ps + 1)
                xpref["done"] = True
            B_sq8 = B_ssq8

            def d_group(v, i):
                nonlocal_itd[0] += 1
                bank = nonlocal_itd[0] % 4
                uo = gbase + 21 + v
                so = uo % NRING

                pre = (v == 0 and i in dfill)
                if pre:
                    bank = dfill[i]

                def fo(h):
                    ins = None
                    for c in (range(15, 16) if pre else range(16)):
                        src = yTa if c < 8 else yTb
                        ins = h.matmul(pb[bank][:, 0:UW], lhsT=src[:, c % 8, i * 128:(i + 1) * 128], rhs=ring[so][:, c, :],
                                       start=(c == 0), stop=(c == 15))
                    return ins
                fw.op("pe", fo, reads=[B_ring[so]] + B_yTa + B_yTb, writes=[B_pb[bank]])
                fw.op("act", lambda h: h.activation(out=sqj2[:], in_=pb[bank][:, 0:UW], func=AF.Square, accum_out=ssq[:, i, v:v + 1]),
                      reads=[B_pb[bank]], writes=[B_sqj2, B_sq8[i]])
                fw.op("dve", lambda h: h.tensor_tensor(out=yout[:, i, v * UW:(v + 1) * UW], in0=pb[bank][:, 0:UW],
                                                       in1=gbc[:, v * UW:(v + 1) * UW], op=ALU.mult),
                      reads=[B_pb[bank], B_gbc], writes=[B_yout[i]] + arenaB[i])
                if nonlocal_itd[0] % 2 == 0:
                    pump()

            def d_final(i):
                Bq = B_sq8[i]
                fw.op("dve", lambda h: h.tensor_reduce(out=ssd[:, i:i + 1], in_=ssq[:, i, :], axis=AX.X, op=ALU.add), reads=[], writes=[Bq])
                fw.op("act", lambda h: h.activation(out=msd[:, i:i + 1], in_=ssd[:, i:i + 1], func=AF.Sqrt, bias=epsc, scale=1.0 / D), reads=[B_eps], writes=[Bq])
                fw.op("dve", lambda h: h.reciprocal(out=rstd2[:, i:i + 1], in_=msd[:, i:i + 1]), reads=[], writes=[Bq])
                if i % 2 == 0:
                    fw.op("dve", lambda h: h.tensor_scalar(out=yout[:, i, :], in0=yout[:, i, :], scalar1=rstd2[:, i:i + 1], scalar2=None, op0=ALU.mult),
                          reads=[Bq], writes=[B_yout[i]])
                else:
                    fw.op("act", lambda h: h.activation(out=yout[:, i, :], in_=yout[:, i, :], func=AF.Copy, scale=rstd2[:, i:i + 1]),
                          reads=[Bq], writes=[B_yout[i]])
                orow = ps * PT + 128 * i
                eng = fw.E["pool"]
                wl = fw._emit_waits(eng, fw._deps([B_yout[i]] + B_ycp, [], []))
                sem, cnt = fw.dma_sems[f"st{i}"]
                cnt += 16
                fw.dma_sems[f"st{i}"] = (sem, cnt)

                def run(h, wl=wl, sem=sem, orow=orow):
                    for (s_, v_) in wl:
                        h.wait_ge(s_, v_)
                    h.dma_start(out=y_d[orow:orow + 128, :], in_=yout[:, i, :], accum_op=ALU.add).then_inc(sem, 16)
                eng.ops.append(run)
                B_yout[i].r.append((sem, cnt, f"dma:st{i}"))

            for v in range(6):
                uo = gbase + 21 + v
                need(uo)
                allow(uo + 3)
                for i in range(8):
                    d_group(v, i)
            need(gbase + 21 + 7)
            allow(gbase + 21 + 7 + 2)
            for i in range(8):
                d_group(6, i)
                d_group(7, i)
                if i >= 1:
                    d_final(i - 1)
            d_final(7)
          fw.fence()

        try:
            emit_all()
        except _Stop:
            fw.fence()
        with nc.Block() as block:
            fw.replay(block)
    return nc


_CACHE = {}


def _host_consts():
    k = np.arange(128)[:, None]
    q = np.arange(128)[None, :]
    mprev = (k > q).astype(np.float32)
    mcur = (k <= q).astype(np.float32)
    zero = np.zeros_like(mprev)
    mask_ss = np.concatenate([mprev, mcur, mprev, mcur], axis=1)
    mask_zs = np.concatenate([zero, mcur, mprev, mcur], axis=1)
    ident = np.eye(128, dtype=np.float32)
    m = np.arange(128)
    sw = np.where((m % 64) < 32, m + 32, m - 32)
    pswap = np.zeros((128, 128), np.float32)
    pswap[sw, m] = 1.0
    half = 32
    inv_freq = (np.float32(10000.0) ** (-np.arange(half, dtype=np.float32) * np.float32(2.0 / 64))).astype(np.float32)
    invf = inv_freq[m % 32].reshape(128, 1).astype(np.float32)
    sgn = np.where((m % 64) < 32, -1.0, 1.0).astype(np.float32).reshape(128, 1)
    return dict(mask_ss=mask_ss, mask_zs=mask_zs, ident=ident, pswap=pswap, invf=invf, sgn=sgn)


def make_in_maps(x, positions, g_pre, w_in, b_qkv, ln_v_g, ln_v_b, w_spatial, b_spatial, attn_sinks, w_out, g_post):
    hc = _host_consts()
    f32 = np.float32
    x = np.asarray(x, f32)
    positions = np.asarray(positions, np.int32)
    w_in0 = np.ascontiguousarray(np.asarray(w_in, f32)[0])
    w_out0 = np.ascontiguousarray(np.asarray(w_out, f32)[0])
    bqkv = np.asarray(b_qkv, f32)[0]
    bq_fm = np.ascontiguousarray(bqkv[0:1024].reshape(8, 128).T)
    bk = bqkv[1024:1152]
    bk_fm = np.ascontiguousarray(np.stack([np.tile(bk[0:64], 2), np.tile(bk[64:128], 2)], axis=1))
    bv = np.ascontiguousarray(bqkv[1152:1280].reshape(1, 128))
    lng_fm = np.ascontiguousarray(np.asarray(ln_v_g, f32)[0].reshape(8, 128).T)
    lnb_fm = np.ascontiguousarray(np.asarray(ln_v_b, f32)[0].reshape(8, 128).T)
    ws = np.asarray(w_spatial, f32)[0]
    wsT = np.ascontiguousarray(np.transpose(ws, (2, 0, 1)).reshape(128, 1024))
    bs = np.ascontiguousarray(np.asarray(b_spatial, f32)[0].reshape(1, 1024))
    sinks = np.asarray(attn_sinks, f32)[0]
    sink_fm = np.zeros((128, 8), f32)
    for ch in range(8):
        sink_fm[0:64, ch] = sinks[2 * ch + 1]
        sink_fm[64:128, ch] = sinks[2 * ch]
    bs_bc = np.broadcast_to(bs, (128, 1024))
    bv_bc = np.broadcast_to(bv, (128, 128))
    gpre_fm = np.asarray(g_pre, f32)[0].reshape(16, 128).T
    cpa = np.ascontiguousarray(np.concatenate([bv_bc, bq_fm, bk_fm, lng_fm, lnb_fm, sink_fm, hc["invf"], hc["sgn"], gpre_fm, bk.reshape(128, 1)], axis=1), dtype=f32)
    cpb_s = np.concatenate([hc["ident"], hc["pswap"], hc["mask_ss"], hc["mask_ss"], wsT, bs_bc], axis=1).astype(f32)
    cpb_z = np.concatenate([hc["ident"], hc["pswap"], hc["mask_ss"], hc["mask_zs"], wsT, bs_bc], axis=1).astype(f32)
    common = dict(w_in=w_in0, w_out=w_out0, g_pre=np.asarray(g_pre, f32)[0].reshape(1, D).copy(),
                  g_post=np.asarray(g_post, f32)[0].reshape(1, D).copy(), cpa=cpa)
    in_maps = []
    for c in range(8):
        b, hf = c // 2, c % 2
        s0 = hf * NTOK
        xs = np.zeros((NTOK + HALO, D), f32)
        xs[HALO:] = x[b, s0:s0 + NTOK]
        ps_ = np.zeros((1, NTOK + HALO), np.int32)
        ps_[0, HALO:] = positions[b, s0:s0 + NTOK]
        if hf == 1:
            xs[:HALO] = x[b, s0 - HALO:s0]
            ps_[0, :HALO] = positions[b, s0 - HALO:s0]
        m = dict(common)
        m["x"] = xs
        m["pos"] = ps_
        m["cpb"] = cpb_s if hf == 1 else cpb_z
        in_maps.append(m)
    return in_maps


def kernel(x, positions, g_pre, w_in, b_qkv, ln_v_g, ln_v_b, w_spatial, b_spatial, attn_sinks, w_out, g_post):
    if "nc" not in _CACHE:
        _CACHE["nc"] = build_program(False)
    nc = _CACHE["nc"]
    in_maps = make_in_maps(x, positions, g_pre, w_in, b_qkv, ln_v_g, ln_v_b, w_spatial, b_spatial, attn_sinks, w_out, g_post)
    res = run_bass_kernel_spmd(nc, in_maps, core_ids=list(range(8)))
    out = np.empty((4, 4096, D), np.float32)
    for c in range(8):
        b, hf = c // 2, c % 2
        out[b, hf * NTOK:(hf + 1) * NTOK] = res.results[c]["y"]
    return out
```
